# Optimizing a Trainium2 kernel written in Bass

```python
import math
import jax, jax.numpy as jnp
from jax import lax
import numpy as np

D_MODEL = 2048
BATCH = 4
SEQ = 4096
DEPTH = 4

N_MIXERS = 2
N_MLA_LAYERS = (DEPTH + N_MIXERS - 1) // N_MIXERS
N_DIL_LAYERS = DEPTH // N_MIXERS

ROPE_THETA = 500000.0
NORM_EPS = 1e-6
ATTN_BLOCK = 128

MLA_HEADS = 16
MLA_Q_RANK = 512
MLA_KV_RANK = 512
MLA_NOPE = 128
MLA_ROPE = 64
MLA_V = 128

DIL_GROUPS = ((128, 1), (512, 4), (2048, 16))
DIL_HEADS = 16
DIL_HEAD_DIM = 128
DIL_ROT = DIL_HEAD_DIM // 4

FFN_HIDDEN = 5632
CONV_WIDTH = 3

kernel_name = "hybrid_mla_dilated_convffn"


def rms_norm(x, g):
    xf = x.astype(jnp.float32)
    y = xf * lax.rsqrt(jnp.mean(xf * xf, axis=-1, keepdims=True) + NORM_EPS)
    return (y * g.astype(jnp.float32)).astype(x.dtype)


def rope(x, positions):
    r = x.shape[-1]
    inv_freq = ROPE_THETA ** (-jnp.arange(0, r, 2, dtype=jnp.float32) / r)
    ang = positions.astype(jnp.float32)[..., None] * inv_freq
    cos = jnp.cos(ang)[:, :, None, :]
    sin = jnp.sin(ang)[:, :, None, :]
    xf = x.astype(jnp.float32)
    x1, x2 = xf[..., : r // 2], xf[..., r // 2:]
    out = jnp.concatenate([x1 * cos - x2 * sin, x2 * cos + x1 * sin], axis=-1)
    return out.astype(x.dtype)


def partial_rope(x, positions):
    return jnp.concatenate([rope(x[..., :DIL_ROT], positions), x[..., DIL_ROT:]], axis=-1)


def dense_causal_attention(q, k, v, scale):
    B, S, H, Dk = q.shape
    nb = S // ATTN_BLOCK
    qb = q.reshape(B, nb, ATTN_BLOCK, H, Dk).transpose(1, 0, 2, 3, 4)
    k_idx = jnp.arange(S)
    starts = jnp.arange(nb) * ATTN_BLOCK

    def one_block(args):
        q_blk, start = args
        s = jnp.einsum('bqhd,bkhd->bhqk', q_blk, k, preferred_element_type=jnp.float32) * scale
        q_idx = start + jnp.arange(ATTN_BLOCK)
        mask = k_idx[None, :] <= q_idx[:, None]
        s = jnp.where(mask, s, -jnp.inf)
        p = jax.nn.softmax(s, axis=-1).astype(v.dtype)
        return jnp.einsum('bhqk,bkhd->bqhd', p, v)

    out = lax.map(one_block, (qb, starts))
    return out.transpose(1, 0, 2, 3, 4).reshape(B, S, H, v.shape[-1])


def mla_mixer(h, positions, wq_a, q_norm, wq_b, wkv_a, kv_norm, wkv_b, wo):
    B, S, _ = h.shape
    q = (rms_norm(h @ wq_a, q_norm) @ wq_b).reshape(B, S, MLA_HEADS, MLA_NOPE + MLA_ROPE)
    q_nope, q_pe = q[..., :MLA_NOPE], rope(q[..., MLA_NOPE:], positions)
    kv_a = h @ wkv_a
    c_kv = kv_a[..., :MLA_KV_RANK]
    k_pe = rope(kv_a[..., None, MLA_KV_RANK:], positions)
    kv = (rms_norm(c_kv, kv_norm) @ wkv_b).reshape(B, S, MLA_HEADS, MLA_NOPE + MLA_V)
    k_nope, v = kv[..., :MLA_NOPE], kv[..., MLA_NOPE:]
    q_full = jnp.concatenate([q_nope, q_pe], axis=-1)
    k_full = jnp.concatenate(
        [k_nope, jnp.broadcast_to(k_pe, (B, S, MLA_HEADS, MLA_ROPE))], axis=-1)
    o = dense_causal_attention(q_full, k_full, v, (MLA_NOPE + MLA_ROPE) ** -0.5)
    return o.reshape(B, S, MLA_HEADS * MLA_V) @ wo


def dilated_group_attention(q, k, v, window, dilation, scale):
    B, S, H, hd = q.shape
    span = window // dilation
    L = S // dilation
    nb = -(-L // ATTN_BLOCK)
    Lp = nb * ATTN_BLOCK
    Q = ATTN_BLOCK

    def to_residue(t):
        t = t.reshape(B, L, dilation, H, hd).transpose(0, 2, 1, 3, 4).reshape(B * dilation, L, H, hd)
        t = jnp.pad(t, ((0, 0), (0, Lp - L), (0, 0), (0, 0)))
        return t.reshape(B * dilation, nb, Q, H, hd)

    def with_prev(t):
        prev = jnp.pad(t[:, :-1], ((0, 0), (1, 0), (0, 0), (0, 0), (0, 0)))
        return jnp.concatenate([prev, t], axis=2)

    qr = to_residue(q)
    kb = with_prev(to_residue(k))
    vb = with_prev(to_residue(v))
    s = jnp.einsum('nbqhd,nbkhd->nbhqk', qr, kb, preferred_element_type=jnp.float32) * scale
    qi = jnp.arange(Q)[:, None]
    ki = jnp.arange(2 * Q)[None, :]
    dist = qi + Q - ki
    band = (dist >= 0) & (dist <= span)
    has_prev = (jnp.arange(nb) > 0)[:, None, None] | (ki >= Q)[None]
    mask = band[None] & has_prev
    s = jnp.where(mask[None, :, None], s, -jnp.inf)
    lse = jax.nn.logsumexp(s, axis=-1)
    p = jnp.exp(s - lse[..., None]).astype(v.dtype)
    o = jnp.einsum('nbhqk,nbkhd->nbqhd', p, vb)
    o = o.reshape(B * dilation, Lp, H, hd)[:, :L]
    o = o.reshape(B, dilation, L, H, hd).transpose(0, 2, 1, 3, 4).reshape(B, S, H, hd)
    lse = lse.transpose(0, 1, 3, 2).reshape(B * dilation, Lp, H)[:, :L]
    lse = lse.reshape(B, dilation, L, H).transpose(0, 2, 1, 3).reshape(B, S, H)
    return o, lse


def dilated_mixer(h, positions, w_in, wo):
    B, S, _ = h.shape
    G = len(DIL_GROUPS)
    qkv = (h @ w_in).reshape(B, S, G, 3, DIL_HEADS, DIL_HEAD_DIM)
    outs, lses = [], []
    for g, (window, dilation) in enumerate(DIL_GROUPS):
        q = partial_rope(qkv[:, :, g, 0], positions)
        k = partial_rope(qkv[:, :, g, 1], positions)
        v = qkv[:, :, g, 2]
        o, l = dilated_group_attention(q, k, v, window, dilation, DIL_HEAD_DIM ** -0.5)
        outs.append(o)
        lses.append(l)
    alpha = jax.nn.softmax(jnp.stack(lses), axis=0)
    o = jnp.einsum('gbsh,gbshd->bshd', alpha, jnp.stack(outs).astype(jnp.float32)).astype(h.dtype)
    return o.reshape(B, S, DIL_HEADS * DIL_HEAD_DIM) @ wo


def conv_ffn(h, w_up, conv_w, conv_b, w_down):
    S = h.shape[1]
    u = h @ w_up
    up = jnp.pad(u, ((0, 0), (CONV_WIDTH - 1, 0), (0, 0)))
    c = conv_b + up[:, 0:S] * conv_w[0]
    for j in range(1, CONV_WIDTH):
        c = c + up[:, j:j + S] * conv_w[j]
    gate, val = c[..., :FFN_HIDDEN], c[..., FFN_HIDDEN:]
    return (jax.nn.silu(gate) * val) @ w_down


def setup_inputs(seed: int = 0) -> dict:
    key = jax.random.key(seed)
    ks = iter(jax.random.split(key, 32))

    def w(shape, fan_in):
        return jax.random.normal(next(ks), shape, jnp.float32) * fan_in ** -0.5

    def gain(shape):
        return 1.0 + 0.01 * jax.random.normal(next(ks), shape, jnp.float32)

    NA, ND = N_MLA_LAYERS, N_DIL_LAYERS
    G = len(DIL_GROUPS)
    return {
        "x": jax.random.normal(next(ks), (BATCH, SEQ, D_MODEL), jnp.float32),
        "positions": jnp.broadcast_to(jnp.arange(SEQ, dtype=jnp.int32), (BATCH, SEQ)),
        "attn_norm": gain((DEPTH, D_MODEL)),
        "ffn_norm": gain((DEPTH, D_MODEL)),
        "final_norm": gain((D_MODEL,)),
        "mla_wq_a": w((NA, D_MODEL, MLA_Q_RANK), D_MODEL),
        "mla_q_norm": gain((NA, MLA_Q_RANK)),
        "mla_wq_b": w((NA, MLA_Q_RANK, MLA_HEADS * (MLA_NOPE + MLA_ROPE)), MLA_Q_RANK),
        "mla_wkv_a": w((NA, D_MODEL, MLA_KV_RANK + MLA_ROPE), D_MODEL),
        "mla_kv_norm": gain((NA, MLA_KV_RANK)),
        "mla_wkv_b": w((NA, MLA_KV_RANK, MLA_HEADS * (MLA_NOPE + MLA_V)), MLA_KV_RANK),
        "mla_wo": w((NA, MLA_HEADS * MLA_V, D_MODEL), MLA_HEADS * MLA_V),
        "dil_w_in": w((ND, D_MODEL, G * 3 * DIL_HEADS * DIL_HEAD_DIM), D_MODEL),
        "dil_wo": w((ND, DIL_HEADS * DIL_HEAD_DIM, D_MODEL), DIL_HEADS * DIL_HEAD_DIM),
        "ffn_w_up": w((DEPTH, D_MODEL, 2 * FFN_HIDDEN), D_MODEL),
        "ffn_conv_w": w((DEPTH, CONV_WIDTH, 2 * FFN_HIDDEN), CONV_WIDTH),
        "ffn_conv_b": 0.01 * jax.random.normal(next(ks), (DEPTH, 2 * FFN_HIDDEN), jnp.float32),
        "ffn_w_down": w((DEPTH, FFN_HIDDEN, D_MODEL), FFN_HIDDEN),
    }


def reference(x, positions, attn_norm, ffn_norm, final_norm,
              mla_wq_a, mla_q_norm, mla_wq_b, mla_wkv_a, mla_kv_norm, mla_wkv_b, mla_wo,
              dil_w_in, dil_wo,
              ffn_w_up, ffn_conv_w, ffn_conv_b, ffn_w_down):
    for i in range(DEPTH):
        h = rms_norm(x, attn_norm[i])
        j = i // N_MIXERS
        if i % N_MIXERS == 0:
            x = x + mla_mixer(h, positions, mla_wq_a[j], mla_q_norm[j], mla_wq_b[j],
                              mla_wkv_a[j], mla_kv_norm[j], mla_wkv_b[j], mla_wo[j])
        else:
            x = x + dilated_mixer(h, positions, dil_w_in[j], dil_wo[j])
        h = rms_norm(x, ffn_norm[i])
        x = x + conv_ffn(h, ffn_w_up[i], ffn_conv_w[i], ffn_conv_b[i], ffn_w_down[i])
    return rms_norm(x, final_norm)
```

```python
import math
from contextlib import ExitStack

import numpy as np
import ml_dtypes

import concourse.bass as bass
import concourse.mybir as mybir
from concourse.bass_utils import run_bass_kernel_spmd

F32 = mybir.dt.float32
BF16 = mybir.dt.bfloat16
I32 = mybir.dt.int32
AF = mybir.ActivationFunctionType
ALU = mybir.AluOpType

D = 2048
DC = D // 128
S = 4096
DEPTH = 4
TT = 512
EPS = 1e-6
THETA = 500000.0
FH = 5632
FC = FH // 128
NB = 4
N_CORES = 8


class Buf:
    __slots__ = ("w", "r", "name")

    def __init__(self, name=""):
        self.w = []
        self.r = {}
        self.name = name


class Eng:
    def __init__(self, name, eng, sem, dma_sems):
        self.name = name
        self.eng = eng
        self.sem = sem
        self.count = 0
        self.seen = {}
        self.dma_sems = dma_sems
        self.dma_cnt = [0] * len(dma_sems)
        self.dma_i = 0


class Sched:
    def __init__(self, nc, es):
        self.nc = nc
        self.bufs = {}

        def sems(prefix, n):
            return [es.enter_context(nc.semaphore(f"{prefix}{i}")) for i in range(n)]

        self.pe = Eng("pe", nc.tensor, sems("s_pe", 1)[0], [])
        self.act = Eng("act", nc.scalar, sems("s_act", 1)[0], sems("d_act", 4))
        self.dve = Eng("dve", nc.vector, sems("s_dve", 1)[0], [])
        self.pool = Eng("pool", nc.gpsimd, sems("s_pool", 1)[0], sems("d_pool", 8))
        self.sp = Eng("sp", nc.sync, sems("s_sp", 1)[0], sems("d_sp", 8))
        self.engs = {e.name: e for e in (self.pe, self.act, self.dve, self.pool, self.sp)}
        self.nwaits = 0

    def B(self, *key):
        b = self.bufs.get(key)
        if b is None:
            b = Buf(str(key))
            self.bufs[key] = b
        return b

    def _wait(self, E, tk):
        sem, val, key = tk
        if E.seen.get(key, 0) >= val:
            return
        if key == "pe" and E.name == "pe":
            return
        src = self.engs.get(key)
        if src is not None and val > src.count:
            raise RuntimeError(f"wait on unsignaled ticket {key}:{val} > {src.count}")
        E.eng.wait_ge(sem, val)
        E.seen[key] = val
        self.nwaits += 1

    def _deps(self, E, reads, writes, add):
        for b in reads:
            for tk in b.w:
                self._wait(E, tk)
        for b in writes:
            if not add:
                for tk in b.w:
                    self._wait(E, tk)
            for tk in b.r.values():
                self._wait(E, tk)

    def _post(self, tk, rkey, reads, writes, add):
        for b in reads:
            b.r[rkey] = tk
        for b in writes:
            if add:
                b.w.append(tk)
            else:
                b.w = [tk]
            b.r = {}

    def op(self, E, fn, reads=(), writes=(), sig=True, add=False):
        self._deps(E, reads, writes, add)
        ins = fn(E.eng)
        if sig:
            E.count += 1
            ins.then_inc(E.sem, 1)
            tk = (E.sem, E.count, E.name)
        else:
            tk = (E.sem, E.count + 1, E.name)
        self._post(tk, E.name, reads, writes, add)
        return ins

    def dma(self, Q, out, in_, reads=(), writes=(), add=False):
        self._deps(Q, reads, writes, add)
        ins = Q.eng.dma_start(out=out, in_=in_)
        i = Q.dma_i % len(Q.dma_sems)
        Q.dma_i += 1
        Q.dma_cnt[i] += 16
        ins.then_inc(Q.dma_sems[i], 16)
        key = f"{Q.name}.d{i}"
        tk = (Q.dma_sems[i], Q.dma_cnt[i], key)
        self._post(tk, key, reads, writes, add)
        return tk

    def wait_all(self, E, bufs):
        for b in bufs:
            for tk in b.w:
                self._wait(E, tk)


def lay_lhsT(W):
    K, N = W.shape
    return np.ascontiguousarray(W.reshape(K // 128, 128, N // 128, 128).transpose(2, 1, 0, 3))


def lay_rhs(W):
    K, N = W.shape
    return np.ascontiguousarray(W.reshape(K // 128, 128, N // 512, 512).transpose(2, 1, 0, 3))


def lay_vec(v):
    return np.ascontiguousarray(v.reshape(-1, 128).T)


def weight_specs(depth):
    specs = []
    for i in range(depth):
        j = i // 2
        if i % 2 == 0:
            specs += [
                (f"L{i}_wqa", [4, 128, 16, 128]),
                (f"L{i}_wqbn", [16, 128, 4, 128]),
                (f"L{i}_wqbp", [8, 128, 4, 128]),
                (f"L{i}_wkva", [4, 128, 16, 128]),
                (f"L{i}_wkpe", [1, 128, 16, 64]),
                (f"L{i}_wkbk", [16, 128, 4, 128]),
                (f"L{i}_wkbv", [4, 128, 4, 512]),
                (f"L{i}_wo", [16, 128, 16, 128]),
            ]
        else:
            specs += [
                (f"L{i}_wqk", [96, 128, 16, 128]),
                (f"L{i}_wv", [12, 128, 16, 512]),
                (f"L{i}_wo", [16, 128, 16, 128]),
            ]
        specs += [
            (f"L{i}_wup", [88, 128, 16, 128]),
            (f"L{i}_wdn", [16, 128, 44, 128]),
        ]
    return specs


def prep_weights(inp, depth):
    out = {}
    for i in range(depth):
        j = i // 2
        if i % 2 == 0:
            out[f"L{i}_wqa"] = lay_lhsT(inp["mla_wq_a"][j])
            wqb = inp["mla_wq_b"][j].reshape(512, 16, 192)
            out[f"L{i}_wqbn"] = lay_lhsT(np.ascontiguousarray(wqb[:, :, :128]).reshape(512, 2048))
            out[f"L{i}_wqbp"] = lay_lhsT(np.ascontiguousarray(wqb[:, :, 128:]).reshape(512, 1024))
            wkva = inp["mla_wkv_a"][j]
            out[f"L{i}_wkva"] = lay_lhsT(np.ascontiguousarray(wkva[:, :512]))
            out[f"L{i}_wkpe"] = np.ascontiguousarray(
                wkva[:, 512:].reshape(16, 128, 1, 64).transpose(2, 1, 0, 3))
            wkvb = inp["mla_wkv_b"][j].reshape(512, 16, 256)
            out[f"L{i}_wkbk"] = lay_lhsT(np.ascontiguousarray(wkvb[:, :, :128]).reshape(512, 2048))
            out[f"L{i}_wkbv"] = lay_rhs(np.ascontiguousarray(wkvb[:, :, 128:]).reshape(512, 2048))
            out[f"L{i}_wo"] = lay_lhsT(inp["mla_wo"][j])
        else:
            win = inp["dil_w_in"][j].reshape(2048, 3, 3, 16, 128)
            wqk = np.ascontiguousarray(win[:, :, 0:2]).reshape(2048, 3 * 2 * 16 * 128)
            out[f"L{i}_wqk"] = lay_lhsT(wqk)
            wv = np.ascontiguousarray(win[:, :, 2]).reshape(2048, 3 * 16 * 128)
            out[f"L{i}_wv"] = lay_rhs(wv)
            out[f"L{i}_wo"] = lay_lhsT(inp["dil_wo"][j])
        out[f"L{i}_wup"] = lay_lhsT(inp["ffn_w_up"][i])
        out[f"L{i}_wdn"] = lay_lhsT(inp["ffn_w_down"][i])
    return out


def const_inputs():
    c = {}
    c["ones_f"] = np.ones((128, 128), np.float32)
    c["ones_b"] = np.ones((128, 128), ml_dtypes.bfloat16)
    pm = np.zeros((128, 128), np.float32)
    for m in range(128):
        blk, r = divmod(m, 64)
        pm[blk * 64 + (r + 32) % 64, m] = 1.0
    c["perm_m"] = pm
    pd = np.zeros((128, 128), np.float32)
    for m in range(32):
        pd[(m + 16) % 32, m] = 1.0
    c["perm_d"] = pd
    two_pi = 2.0 * math.pi * (1.0 - 2e-7)
    rm = np.zeros((128, 4), np.float32)
    for p in range(128):
        r = p % 64
        i = r % 32
        rm[p, 0] = (THETA ** (-(2.0 * i) / 64.0)) / (2.0 * math.pi)
        rm[p, 1] = -two_pi if r < 32 else two_pi
        rm[p, 2] = two_pi
    c["rope_m"] = rm
    rd = np.zeros((128, 4), np.float32)
    for p in range(128):
        if p < 32:
            i = p % 16
            rd[p, 0] = (THETA ** (-(2.0 * i) / 32.0)) / (2.0 * math.pi)
            rd[p, 1] = -two_pi if p < 16 else two_pi
        else:
            rd[p, 0] = 0.0
            rd[p, 1] = two_pi
        rd[p, 2] = two_pi
    c["rope_d"] = rd
    k = np.arange(128)[:, None]
    q = np.arange(512)[None, :]
    mm = np.stack([((128 * a + k) <= q) for a in range(4)], axis=1)
    c["mask_m"] = np.ascontiguousarray(mm).astype(ml_dtypes.bfloat16)
    kk = np.arange(128)[:, None]
    qq = np.arange(128)[None, :]
    md = np.stack([kk >= qq, kk <= qq], axis=1)
    c["mask_d"] = np.ascontiguousarray(md).astype(ml_dtypes.bfloat16)
    md0 = md.copy()
    md0[:, 0, :] = False
    c["mask_d0"] = np.ascontiguousarray(md0).astype(ml_dtypes.bfloat16)
    return c


CONST_SPECS = [
    ("ones_f", [128, 128], F32), ("ones_b", [128, 128], BF16),
    ("perm_m", [128, 128], F32), ("perm_d", [128, 128], F32),
    ("rope_m", [128, 4], F32), ("rope_d", [128, 4], F32),
    ("mask_m", [128, 4, 512], BF16), ("mask_d", [128, 2, 128], BF16), ("mask_d0", [128, 2, 128], BF16),
]


def build_program(depth=DEPTH, T=S, final_norm=True, stop_after=None):
    NT = T // TT
    NKB = T // 128
    nc = bass.Bass("TRN2", target_bir_lowering=False)
    es = ExitStack()
    dram_in = {}

    def din(name, shape, dt):
        dram_in[name] = nc.dram_tensor(name, shape, dt, kind="ExternalInput").ap()
        return dram_in[name]

    def dscr(name, shape, dt):
        return nc.dram_tensor(name, shape, dt, kind="Internal").ap()

    x_in = din("xT", [DC, 128, T], F32)
    pos_in = din("pos", [128, T], I32)
    g_attn = din("g_attn", [128, DEPTH, DC], F32)
    g_ffn = din("g_ffn", [128, DEPTH, DC], F32)
    g_fin = din("g_fin", [128, DC], F32)
    g_qn = din("g_qn", [128, 2, 4], F32)
    g_kvn = din("g_kvn", [128, 2, 4], F32)
    cw_in = din("conv_w", [128, DEPTH, 3, 88], F32)
    cb_in = din("conv_b", [128, DEPTH, 88], F32)
    for name, shape, dt in CONST_SPECS:
        din(name, shape, dt)
    wspecs = weight_specs(depth)
    w32 = {}
    w16 = {}
    for name, shape in wspecs:
        w32[name] = din(name, shape, F32)
        w16[name] = dscr(name + "_bf", shape, BF16)
    y_out = nc.dram_tensor("yT", [DC, 128, T], F32, kind="ExternalOutput").ap()

    xs = dscr("xs", [DC, 128, T], F32)
    sc_qn = dscr("sc_qn", [16, 128, T], BF16)
    sc_qp = dscr("sc_qp", [16, 64, T], BF16)
    sc_kn = dscr("sc_kn", [16, 128, T], BF16)
    sc_kp = dscr("sc_kp", [64, T], BF16)
    sc_v = dscr("sc_v", [16, 128, T // 128, 128], BF16)
    sc_o = dscr("sc_o", [16, 128, T], BF16)
    sc_dq = dscr("sc_dq", [3, 16, 128, T], BF16)
    sc_dk = dscr("sc_dk", [3, 16, 128, T], BF16)
    sc_dv = dscr("sc_dv", [3, 16, T, 128], BF16)

    sch = Sched(nc, es)
    state = {"bank": 0, "w": 0, "ev": 0, "evb": 0, "phase": 0, "bank2": 0}
    PE, ACT, DVE, POOL, SP = sch.pe, sch.act, sch.dve, sch.pool, sch.sp
    B = sch.B

    def sb(name, shape, dt):
        return es.enter_context(nc.sbuf_tensor(name, shape, dt))

    ht = sb("ht", [128, DC, TT], BF16)
    NWB = 3
    wst = [sb(f"wst{i}", [128, 2048], BF16) for i in range(NWB)]
    sqb = [sb(f"sqb{i}", [128, TT], F32) for i in range(2)]
    rsb = sb("rsb", [128, TT], F32)
    NEV = 3
    evf = [sb(f"evf{i}", [128, TT + 2], F32) for i in range(NEV)]
    evc = [sb(f"evc{i}", [128, TT], F32) for i in range(NEV)]
    evb = [sb(f"evb{i}", [128, TT], BF16) for i in range(4)]
    uhalo = sb("uhalo", [128, 88, 2], F32)
    cgate = sb("cgate", [128, TT], F32)
    apt = [sb(f"apt{i}", [128, TT], BF16) for i in range(3)]
    arec = sb("arec", [128, TT], F32)
    P = {}

    def phase_alloc(pes, kind):
        def a(name, shape, dt):
            P[name] = pes.enter_context(nc.sbuf_tensor(name + "_" + kind + str(state["phase"]), shape, dt))
        state["phase"] += 1
        if kind in ("A",):
            a("xt", [128, DC, TT], F32)
            a("small", [128, 4, TT], F32)
            a("smallb", [128, 4, TT], BF16)
            a("posi", [128, TT], I32)
            a("rt_u", [128, TT], F32)
            a("rt_i", [128, TT], I32)
            a("rt_f", [128, TT], F32)
            a("rt_m", [128, TT], F32)
            a("tabC", [128, TT], F32)
            a("tabS", [128, TT], F32)
        elif kind == "C":
            a("xt", [128, DC, TT], F32)
            a("gt", [128, FC, TT], BF16)
        elif kind == "BM":
            for i in range(2):
                a(f"akn{i}", [128, T], BF16)
                a(f"akv{i}", [128, NKB, 128], BF16)
                a(f"aqn{i}", [128, TT], BF16)
                a(f"aqp{i}", [64, TT], BF16)
            a("akp", [64, T], BF16)
        elif kind == "BD":
            for i in range(2):
                a(f"akn{i}", [128, T], BF16)
                a(f"akv{i}", [128, NKB, 128], BF16)
                a(f"aqf{i}", [128, T], BF16)
            a("oacc", [128, T], F32)
            a("dacc", [128, T], F32)

    def barrier():
        tks = []
        for E in (PE, ACT, DVE, POOL):
            if E.count > 0:
                tks.append((E.sem, E.count, E.name))
        for Q in (SP, POOL, ACT):
            for i, sem in enumerate(Q.dma_sems):
                if Q.dma_cnt[i] > 0:
                    tks.append((sem, Q.dma_cnt[i], f"{Q.name}.d{i}"))
        for E in (PE, ACT, DVE, POOL, SP):
            for tk in tks:
                sch._wait(E, tk)

    cst = {}
    for name, shape, dt in CONST_SPECS:
        cst[name] = sb("c_" + name, shape, dt)
    s_gattn = sb("s_gattn", [128, DEPTH, DC], F32)
    s_gffn = sb("s_gffn", [128, DEPTH, DC], F32)
    s_gfin = sb("s_gfin", [128, DC], F32)
    s_gqn = sb("s_gqn", [128, 2, 4], F32)
    s_gkvn = sb("s_gkvn", [128, 2, 4], F32)
    s_cw = sb("s_cw", [128, DEPTH, 3, 88], F32)
    s_cb = sb("s_cb", [128, DEPTH, 88], F32)
    ps = es.enter_context(nc.psum_tensor("ps", [128, 8, 512], F32))
    PS = [B("ps", i) for i in range(8)]

    def next_bank(lo=0, hi=8):
        b = state["bank"]
        if b < lo or b >= hi:
            b = lo
        state["bank"] = b + 1 if b + 1 < hi else lo
        return b

    CB = B("consts")
    for name, shape, dt in CONST_SPECS:
        sch.dma(SP, cst[name][:], dram_in[name], writes=[B("c", name)])
    for dst, src, nm in ((s_gattn, g_attn, "ga"), (s_gffn, g_ffn, "gf"), (s_gfin, g_fin, "gn"),
                         (s_gqn, g_qn, "gq"), (s_gkvn, g_kvn, "gk"), (s_cw, cw_in, "cw"), (s_cb, cb_in, "cb")):
        sch.dma(SP, dst[:], src, writes=[B("c", nm)])
    CONSTS = [B("c", n) for n, _, _ in CONST_SPECS] + [B("c", n) for n in ("ga", "gf", "gn", "gq", "gk", "cw", "cb")]
    for E in (PE, ACT, DVE, POOL):
        sch.wait_all(E, CONSTS)

    for name, shape in wspecs:
        n0 = shape[0]
        per = shape[1] * shape[2] * shape[3]
        grp = max(1, (1 << 20) // per)
        for o0 in range(0, n0, grp):
            o1 = min(n0, o0 + grp)
            sch.dma(POOL, w16[name][o0:o1], w32[name][o0:o1], writes=[B("w", name, o0 // grp)])
    wgrp = {name: max(1, (1 << 20) // (shape[1] * shape[2] * shape[3])) for name, shape in wspecs}

    def wbuf(name, oc):
        return B("w", name, oc // wgrp[name])

    def load_stage(name, oc, k0, kn, width=128):
        i = state["w"] % NWB
        state["w"] += 1
        view = wst[i][:, 0:kn * width].rearrange("p (k w) -> p k w", w=width)
        sch.dma(SP, view, w16[name][oc, :, k0:k0 + kn, :], reads=[wbuf(name, oc)], writes=[B("wst", i)])
        return view, B("wst", i)

    def proj_fm(name, n_oc, KC, rhs_fn, rhs_bufs, evac, width=128, bank_lo=0, bank_hi=8):
        for oc in range(n_oc):
            bk = next_bank(bank_lo, bank_hi)
            k0 = 0
            while k0 < KC:
                kn = min(16, KC - k0)
                wv, wb = load_stage(name, oc, k0, kn, width)
                for kk in range(kn):
                    kc = k0 + kk
                    last = kc == KC - 1
                    sch.op(PE, lambda e, kk=kk, kc=kc, last=last: e.matmul(
                        ps[0:width, bk, :], lhsT=wv[:, kk, :], rhs=rhs_fn(kc), start=(kc == 0), stop=last),
                        reads=[wb] + rhs_bufs, writes=[PS[bk]], sig=last)
                k0 += kn
            evac(oc, ps[0:width, bk, :], PS[bk])

    def rmsnorm(src_fn, src_bufs, nch, dim, gain_ap_fn, dst_fn, dst_bufs):
        bk = next_bank()
        for c in range(nch):
            q = sqb[c % 2]
            sch.op(ACT, lambda e, c=c, q=q: e.activation(out=q[:], in_=src_fn(c), func=AF.Square),
                   reads=src_bufs, writes=[B("sqb", c % 2)])
            sch.op(PE, lambda e, c=c, q=q: e.matmul(ps[:, bk, :], lhsT=cst["ones_f"][:], rhs=q[:],
                                                    start=(c == 0), stop=(c == nch - 1)),
                   reads=[B("sqb", c % 2)], writes=[PS[bk]], sig=True)
        sch.op(ACT, lambda e: e.activation(out=rsb[:], in_=ps[:, bk, :], func=AF.Sqrt, bias=EPS_AP[:], scale=1.0 / dim),
               reads=[PS[bk]], writes=[B("rsb")])
        sch.op(DVE, lambda e: e.reciprocal(out=rsb[:], in_=rsb[:]), reads=[B("rsb")], writes=[B("rsb")])
        for c in range(nch):
            sch.op(DVE, lambda e, c=c: e.scalar_tensor_tensor(
                out=dst_fn(c), in0=src_fn(c), scalar=gain_ap_fn(c), in1=rsb[:], op0=ALU.mult, op1=ALU.mult),
                reads=src_bufs + [B("rsb")], writes=dst_bufs)

    eps_t = sb("eps_t", [128, 1], F32)
    EPS_AP = eps_t
    sch.op(DVE, lambda e: e.memset(eps_t[:], EPS), writes=[B("eps")])
    sch.op(DVE, lambda e: e.memset(uhalo[:], 0.0), writes=[B("uhalo")])
    for E in (ACT, POOL, PE):
        sch.wait_all(E, [B("eps")])

    def rope_tables(t, kind):
        rc = cst["rope_m"] if kind == "m" else cst["rope_d"]
        sch.dma(SP, P["posi"][:], pos_in[:, t * TT:(t + 1) * TT], writes=[B("posi")])
        sch.op(DVE, lambda e: e.tensor_copy(out=P["rt_f"][:], in_=P["posi"][:]), reads=[B("posi")], writes=[B("rt_f")])
        for which in (0, 1):
            off = 0.25 if which == 0 else 0.0
            dst = P["tabC"] if which == 0 else P["tabS"]
            dstb = B("tabC") if which == 0 else B("tabS")
            sch.op(DVE, lambda e: e.tensor_scalar(out=P["rt_u"][:], in0=P["rt_f"][:], scalar1=rc[:, 0:1], scalar2=off,
                                                  op0=ALU.mult, op1=ALU.add),
                   reads=[B("rt_f")], writes=[B("rt_u")])
            sch.op(DVE, lambda e: e.tensor_copy(out=P["rt_i"][:], in_=P["rt_u"][:]), reads=[B("rt_u")], writes=[B("rt_i")])
            sch.op(DVE, lambda e: e.tensor_copy(out=P["rt_m"][:], in_=P["rt_i"][:]), reads=[B("rt_i")], writes=[B("rt_m")])
            sch.op(DVE, lambda e: e.tensor_tensor(out=P["rt_u"][:], in0=P["rt_u"][:], in1=P["rt_m"][:], op=ALU.subtract),
                   reads=[B("rt_u"), B("rt_m")], writes=[B("rt_u")])
            sch.op(DVE, lambda e: e.tensor_single_scalar(out=P["rt_m"][:], in_=P["rt_u"][:], scalar=0.5, op=ALU.is_gt),
                   reads=[B("rt_u")], writes=[B("rt_m")])
            sch.op(DVE, lambda e: e.tensor_tensor(out=P["rt_u"][:], in0=P["rt_u"][:], in1=P["rt_m"][:], op=ALU.subtract),
                   reads=[B("rt_u"), B("rt_m")], writes=[B("rt_u")])
            sch.op(DVE, lambda e: e.tensor_single_scalar(out=P["rt_m"][:], in_=P["rt_u"][:], scalar=-0.5, op=ALU.is_lt),
                   reads=[B("rt_u")], writes=[B("rt_m")])
            sch.op(DVE, lambda e: e.tensor_tensor(out=P["rt_u"][:], in0=P["rt_u"][:], in1=P["rt_m"][:], op=ALU.add),
                   reads=[B("rt_u"), B("rt_m")], writes=[B("rt_u")])
            sc = rc[:, 2:3] if which == 0 else rc[:, 1:2]
            sch.op(ACT, lambda e, dst=dst, sc=sc: e.activation(out=dst[:], in_=P["rt_u"][:], func=AF.Sin, scale=sc),
                   reads=[B("rt_u")], writes=[dstb])

    def rope_apply(src_ps, src_buf, nparts, perm, out_bf, out_buf):
        i = state["ev"] % NEV
        state["ev"] += 1
        xf = evf[i]
        xc = evc[i]
        sch.op(ACT, lambda e: e.activation(out=xf[0:nparts, 0:TT], in_=src_ps, func=AF.Copy),
               reads=[src_buf], writes=[B("evf", i)])
        bk = next_bank()
        sch.op(PE, lambda e: e.matmul(ps[0:nparts, bk, :], lhsT=perm[0:nparts, 0:nparts], rhs=xf[0:nparts, 0:TT],
                                      start=True, stop=True),
               reads=[B("evf", i)], writes=[PS[bk]], sig=True)
        sch.op(DVE, lambda e: e.tensor_tensor(out=xc[0:nparts, :], in0=xf[0:nparts, 0:TT], in1=P["tabC"][0:nparts, :], op=ALU.mult),
               reads=[B("evf", i), B("tabC")], writes=[B("evc", i)])
        sch.op(DVE, lambda e: e.tensor_tensor(out=xf[0:nparts, 0:TT], in0=ps[0:nparts, bk, :], in1=P["tabS"][0:nparts, :], op=ALU.mult),
               reads=[PS[bk], B("tabS")], writes=[B("evf", i)])
        sch.op(DVE, lambda e: e.tensor_tensor(out=out_bf, in0=xc[0:nparts, :], in1=xf[0:nparts, 0:TT], op=ALU.add),
               reads=[B("evf", i), B("evc", i)], writes=[out_buf])

    def next_evb():
        i = state["evb"] % 4
        state["evb"] += 1
        return i

    def load_x(layer, t):
        src = x_in if layer == 0 else xs
        sch.dma(SP, P["xt"][:], src[:, :, t * TT:(t + 1) * TT].rearrange("c p t -> p c t"),
                reads=[B("xs", t)], writes=[B("xt")])

    def evac_to_scratch(dst_ap, dst_buf, nparts=128):
        def f(oc, pap, pbuf):
            i = next_evb()
            sch.op(ACT, lambda e: e.activation(out=evb[i][0:nparts, :], in_=pap, func=AF.Copy),
                   reads=[pbuf], writes=[B("evb", i)])
            sch.dma(POOL, dst_ap(oc), evb[i][0:nparts, :], reads=[B("evb", i)], writes=[dst_buf(oc)])
        return f

    def mla_phase_a(layer):
        j = layer // 2
        for t in range(NT):
            tsl = slice(t * TT, (t + 1) * TT)
            load_x(layer, t)
            rmsnorm(lambda c: P["xt"][:, c, :], [B("xt")], DC, D, lambda c: s_gattn[:, layer, c:c + 1],
                    lambda c: ht[:, c, :], [B("ht")])
            rope_tables(t, "m")
            def ev_small(oc, pap, pbuf):
                sch.op(ACT, lambda e: e.activation(out=P["small"][:, oc, :], in_=pap, func=AF.Copy),
                       reads=[pbuf], writes=[B("small")])
            proj_fm(f"L{layer}_wqa", 4, 16, lambda kc: ht[:, kc, :], [B("ht")], ev_small)
            rmsnorm(lambda c: P["small"][:, c, :], [B("small")], 4, 512, lambda c: s_gqn[:, j, c:c + 1],
                    lambda c: P["smallb"][:, c, :], [B("smallb")])
            proj_fm(f"L{layer}_wqbn", 16, 4, lambda kc: P["smallb"][:, kc, :], [B("smallb")],
                    evac_to_scratch(lambda oc: sc_qn[oc, :, tsl], lambda oc: B("sc_qn", oc, t)))
            def ev_qpe(oc, pap, pbuf):
                i = next_evb()
                rope_apply(pap, pbuf, 128, cst["perm_m"], evb[i][:], B("evb", i))
                for hh in range(2):
                    sch.dma(POOL, sc_qp[2 * oc + hh, :, tsl], evb[i][64 * hh:64 * hh + 64, :],
                            reads=[B("evb", i)], writes=[B("sc_qp", 2 * oc + hh, t)])
            proj_fm(f"L{layer}_wqbp", 8, 4, lambda kc: P["smallb"][:, kc, :], [B("smallb")], ev_qpe)
            proj_fm(f"L{layer}_wkva", 4, 16, lambda kc: ht[:, kc, :], [B("ht")], ev_small)
            def ev_kpe(oc, pap, pbuf):
                i = next_evb()
                rope_apply(pap, pbuf, 64, cst["perm_m"], evb[i][0:64, :], B("evb", i))
                sch.dma(POOL, sc_kp[:, tsl], evb[i][0:64, :], reads=[B("evb", i)], writes=[B("sc_kp", t)])
            proj_fm(f"L{layer}_wkpe", 1, 16, lambda kc: ht[:, kc, :], [B("ht")], ev_kpe, width=64)
            rmsnorm(lambda c: P["small"][:, c, :], [B("small")], 4, 512, lambda c: s_gkvn[:, j, c:c + 1],
                    lambda c: P["smallb"][:, c, :], [B("smallb")])
            proj_fm(f"L{layer}_wkbk", 16, 4, lambda kc: P["smallb"][:, kc, :], [B("smallb")],
                    evac_to_scratch(lambda oc: sc_kn[oc, :, tsl], lambda oc: B("sc_kn", oc, t)))
            name = f"L{layer}_wkbv"
            for cb in range(4):
                wv, wb = load_stage(name, cb, 0, 4, width=512)
                for sbk in range(4):
                    bk = next_bank()
                    for kc in range(4):
                        sch.op(PE, lambda e, kc=kc, sbk=sbk: e.matmul(
                            ps[:, bk, :], lhsT=P["smallb"][:, kc, sbk * 128:(sbk + 1) * 128], rhs=wv[:, kc, :],
                            start=(kc == 0), stop=(kc == 3)),
                            reads=[wb, B("smallb")], writes=[PS[bk]], sig=(kc == 3))
                    i = next_evb()
                    sch.op(ACT, lambda e, i=i: e.activation(out=evb[i][:], in_=ps[:, bk, :], func=AF.Copy),
                           reads=[PS[bk]], writes=[B("evb", i)])
                    r0 = t * TT + sbk * 128
                    sch.dma(POOL, sc_v[4 * cb:4 * cb + 4, :, r0 // 128, :].rearrange("h p j -> p h j"),
                            evb[i][:].rearrange("p (h j) -> p h j", j=128),
                            reads=[B("evb", i)], writes=[B("sc_v", t, cb, sbk)])

    def mla_phase_b(layer):
        scale = (128 + 64) ** -0.5
        sch.dma(SP, P["akp"][:], sc_kp[:, :], reads=[B("sc_kp", t) for t in range(NT)], writes=[B("akp")])
        for h in range(16):
            kb_i = h % 2
            sch.dma(SP, P[f"akn{kb_i}"][:], sc_kn[h, :, :], reads=[B("sc_kn", h, t) for t in range(NT)],
                    writes=[B("akn", kb_i)])
            sch.dma(SP, P[f"akv{kb_i}"][:], sc_v[h],
                    reads=[B("sc_v", t, h // 4, s_) for t in range(NT) for s_ in range(4)],
                    writes=[B("akv", kb_i)])
            for qi in range(NT):
                qb = qi % 2
                qsl = slice(qi * TT, (qi + 1) * TT)
                sch.dma(SP, P[f"aqn{qb}"][:], sc_qn[h, :, qsl], reads=[B("sc_qn", h, qi)], writes=[B("aqn", qb)])
                sch.dma(SP, P[f"aqp{qb}"][:], sc_qp[h, :, qsl], reads=[B("sc_qp", h, qi)], writes=[B("aqp", qb)])
                ob = 4 + (qi % 2)
                db = 6 + (qi % 2)
                nkb = 4 * qi + 4
                for kb in range(nkb):
                    bk = next_bank(0, 4)
                    ksl = slice(kb * 128, (kb + 1) * 128)
                    sch.op(PE, lambda e: e.matmul(ps[:, bk, :], lhsT=P[f"akn{kb_i}"][:, ksl], rhs=P[f"aqn{qb}"][:],
                                                  start=True, stop=False),
                           reads=[B("akn", kb_i), B("aqn", qb)], writes=[PS[bk]], sig=False)
                    sch.op(PE, lambda e: e.matmul(ps[:, bk, :], lhsT=P["akp"][:, ksl], rhs=P[f"aqp{qb}"][:],
                                                  start=False, stop=True),
                           reads=[B("akp"), B("aqp", qb)], writes=[PS[bk]], sig=True)
                    pi = (kb % 3)
                    sch.op(ACT, lambda e: e.activation(out=apt[pi][:], in_=ps[:, bk, :], func=AF.Exp, scale=scale),
                           reads=[PS[bk]], writes=[B("apt", pi)])
                    a = kb - 4 * qi
                    if a >= 0:
                        sch.op(DVE, lambda e: e.tensor_tensor(out=apt[pi][:], in0=apt[pi][:], in1=cst["mask_m"][:, a, :],
                                                              op=ALU.mult),
                               reads=[B("apt", pi)], writes=[B("apt", pi)])
                    last = kb == nkb - 1
                    sch.op(PE, lambda e: e.matmul(ps[:, ob, :], lhsT=P[f"akv{kb_i}"][:, kb, :], rhs=apt[pi][:],
                                                  start=(kb == 0), stop=last),
                           reads=[B("akv", kb_i), B("apt", pi)], writes=[PS[ob]], sig=False)
                    sch.op(PE, lambda e: e.matmul(ps[:, db, :], lhsT=cst["ones_b"][:], rhs=apt[pi][:],
                                                  start=(kb == 0), stop=last),
                           reads=[B("apt", pi)], writes=[PS[db]], sig=True)
                sch.op(DVE, lambda e: e.reciprocal(out=arec[:], in_=ps[:, db, :]), reads=[PS[db]], writes=[B("arec")])
                i = next_evb()
                sch.op(DVE, lambda e: e.tensor_tensor(out=evb[i][:], in0=ps[:, ob, :], in1=arec[:], op=ALU.mult),
                       reads=[PS[ob], B("arec")], writes=[B("evb", i)])
                sch.dma(POOL, sc_o[h, :, qsl], evb[i][:], reads=[B("evb", i)], writes=[B("sc_o", h, qi)])

    def phase_c(layer, wo_name, last_layer):
        for t in range(NT):
            tsl = slice(t * TT, (t + 1) * TT)
            load_x(layer, t)
            sch.dma(SP, ht[:], sc_o[:, :, tsl].rearrange("c p t -> p c t"),
                    reads=[B("sc_o", h, t) for h in range(16)], writes=[B("ht")])
            def ev_res(oc, pap, pbuf):
                sch.op(DVE, lambda e: e.tensor_tensor(out=P["xt"][:, oc, :], in0=pap, in1=P["xt"][:, oc, :], op=ALU.add),
                       reads=[pbuf, B("xt")], writes=[B("xt")])
            proj_fm(wo_name, 16, 16, lambda kc: ht[:, kc, :], [B("ht")], ev_res)
            rmsnorm(lambda c: P["xt"][:, c, :], [B("xt")], DC, D, lambda c: s_gffn[:, layer, c:c + 1],
                    lambda c: ht[:, c, :], [B("ht")])
            name = f"L{layer}_wup"

            def conv_chunk(oc, pap, pbuf, dst, dst_buf):
                i = state["ev"] % NEV
                state["ev"] += 1
                u = evf[i]
                ub = B("evf", i)
                sch.op(POOL, lambda e: e.tensor_copy(out=u[:, 0:2], in_=uhalo[:, oc, :]),
                       reads=[B("uhalo")], writes=[ub])
                sch.op(ACT, lambda e: e.activation(out=u[:, 2:TT + 2], in_=pap, func=AF.Copy),
                       reads=[pbuf], writes=[ub])
                sch.op(POOL, lambda e: e.tensor_copy(out=uhalo[:, oc, :], in_=u[:, TT:TT + 2]),
                       reads=[ub], writes=[B("uhalo")])
                sch.op(ACT, lambda e: e.activation(out=dst, in_=u[:, 2:TT + 2], func=AF.Identity,
                                                   bias=s_cb[:, layer, oc:oc + 1], scale=s_cw[:, layer, 2, oc:oc + 1]),
                       reads=[ub], writes=[dst_buf])
                sch.op(DVE, lambda e: e.scalar_tensor_tensor(out=dst, in0=u[:, 1:TT + 1], scalar=s_cw[:, layer, 1, oc:oc + 1],
                                                             in1=dst, op0=ALU.mult, op1=ALU.add),
                       reads=[ub, dst_buf], writes=[dst_buf])
                sch.op(DVE, lambda e: e.scalar_tensor_tensor(out=dst, in0=u[:, 0:TT], scalar=s_cw[:, layer, 0, oc:oc + 1],
                                                             in1=dst, op0=ALU.mult, op1=ALU.add),
                       reads=[ub, dst_buf], writes=[dst_buf])

            for jc in range(FC):
                res = {}
                for which, oc in (("g", jc), ("v", FC + jc)):
                    bk = next_bank()
                    wv, wb = load_stage(name, oc, 0, 16)
                    for kc in range(16):
                        sch.op(PE, lambda e, kc=kc: e.matmul(ps[:, bk, :], lhsT=wv[:, kc, :], rhs=ht[:, kc, :],
                                                             start=(kc == 0), stop=(kc == 15)),
                               reads=[wb, B("ht")], writes=[PS[bk]], sig=(kc == 15))
                    res[which] = bk
                if t == 0:
                    pass
                conv_chunk(jc, ps[:, res["g"], :], PS[res["g"]], cgate[:], B("cgate"))
                sch.op(ACT, lambda e: e.activation(out=cgate[:], in_=cgate[:], func=AF.Silu),
                       reads=[B("cgate")], writes=[B("cgate")])
                i = state["ev"] % NEV
                vb = evc[i]
                conv_chunk(FC + jc, ps[:, res["v"], :], PS[res["v"]], vb[:], B("evc", i))
                sch.op(DVE, lambda e: e.tensor_tensor(out=P["gt"][:, jc, :], in0=cgate[:], in1=vb[:], op=ALU.mult),
                       reads=[B("cgate"), B("evc", i)], writes=[B("gt")])
            def ev_res2(oc, pap, pbuf):
                sch.op(DVE, lambda e: e.tensor_tensor(out=P["xt"][:, oc, :], in0=pap, in1=P["xt"][:, oc, :], op=ALU.add),
                       reads=[pbuf, B("xt")], writes=[B("xt")])
            proj_fm(f"L{layer}_wdn", 16, FC, lambda kc: P["gt"][:, kc, :], [B("gt")], ev_res2)
            if last_layer and final_norm:
                bk = next_bank()
                for c in range(DC):
                    q = sqb[c % 2]
                    sch.op(ACT, lambda e, c=c, q=q: e.activation(out=q[:], in_=P["xt"][:, c, :], func=AF.Square),
                           reads=[B("xt")], writes=[B("sqb", c % 2)])
                    sch.op(PE, lambda e, c=c, q=q: e.matmul(ps[:, bk, :], lhsT=cst["ones_f"][:], rhs=q[:],
                                                            start=(c == 0), stop=(c == DC - 1)),
                           reads=[B("sqb", c % 2)], writes=[PS[bk]], sig=True)
                sch.op(ACT, lambda e: e.activation(out=rsb[:], in_=ps[:, bk, :], func=AF.Sqrt, bias=EPS_AP[:], scale=1.0 / D),
                       reads=[PS[bk]], writes=[B("rsb")])
                sch.op(DVE, lambda e: e.reciprocal(out=rsb[:], in_=rsb[:]), reads=[B("rsb")], writes=[B("rsb")])
                for c in range(DC):
                    sch.op(DVE, lambda e, c=c: e.scalar_tensor_tensor(
                        out=P["xt"][:, c, :], in0=P["xt"][:, c, :], scalar=s_gfin[:, c:c + 1], in1=rsb[:],
                        op0=ALU.mult, op1=ALU.mult), reads=[B("xt"), B("rsb")], writes=[B("xt")])
                sch.dma(SP, y_out[:, :, tsl].rearrange("c p t -> p c t"), P["xt"][:], reads=[B("xt")], writes=[B("y", t)])
            elif last_layer:
                sch.dma(SP, y_out[:, :, tsl].rearrange("c p t -> p c t"), P["xt"][:], reads=[B("xt")], writes=[B("y", t)])
            else:
                sch.dma(SP, xs[:, :, tsl].rearrange("c p t -> p c t"), P["xt"][:], reads=[B("xt")], writes=[B("xs", t)])
        sch.op(DVE, lambda e: e.memset(uhalo[:], 0.0), writes=[B("uhalo")])

    DIL = (1, 4, 16)

    def dil_phase_a(layer):
        for t in range(NT):
            tsl = slice(t * TT, (t + 1) * TT)
            load_x(layer, t)
            rmsnorm(lambda c: P["xt"][:, c, :], [B("xt")], DC, D, lambda c: s_gattn[:, layer, c:c + 1],
                    lambda c: ht[:, c, :], [B("ht")])
            rope_tables(t, "d")
            def ev_qk(oc, pap, pbuf):
                g, rem = divmod(oc, 32)
                qk, h = divmod(rem, 16)
                i = next_evb()
                rope_apply(pap, pbuf, 128, cst["perm_d"], evb[i][:], B("evb", i))
                dst = sc_dq if qk == 0 else sc_dk
                sch.dma(POOL, dst[g, h, :, tsl], evb[i][:], reads=[B("evb", i)],
                        writes=[B("sc_dq" if qk == 0 else "sc_dk", g, h, t)])
            proj_fm(f"L{layer}_wqk", 96, 16, lambda kc: ht[:, kc, :], [B("ht")], ev_qk)
            name = f"L{layer}_wv"
            for cb in range(12):
                g, hb = divmod(cb, 4)
                wv, wb = load_stage(name, cb, 0, 4, width=512)
                wv2, wb2 = load_stage(name, cb, 4, 4, width=512)
                wv3, wb3 = load_stage(name, cb, 8, 4, width=512)
                pend = []
                for sbk in range(4):
                    bk = next_bank()
                    for kc in range(12):
                        wvx, wbx = ((wv, wb), (wv2, wb2), (wv3, wb3))[kc // 4]
                        sch.op(PE, lambda e, kc=kc, sbk=sbk, wvx=wvx: e.matmul(
                            ps[:, bk, :], lhsT=ht[:, kc, sbk * 128:(sbk + 1) * 128], rhs=wvx[:, kc % 4, :],
                            start=(kc == 0), stop=False),
                            reads=[wbx, B("ht")], writes=[PS[bk]], sig=(kc == 11))
                    pend.append(bk)
                wv4, wb4 = load_stage(name, cb, 12, 4, width=512)
                for sbk in range(4):
                    bk = pend[sbk]
                    for kc in range(12, 16):
                        sch.op(PE, lambda e, kc=kc, sbk=sbk: e.matmul(
                            ps[:, bk, :], lhsT=ht[:, kc, sbk * 128:(sbk + 1) * 128], rhs=wv4[:, kc % 4, :],
                            start=False, stop=(kc == 15)),
                            reads=[wb4, B("ht")], writes=[PS[bk]], sig=(kc == 15))
                    i = next_evb()
                    sch.op(ACT, lambda e, i=i: e.activation(out=evb[i][:], in_=ps[:, bk, :], func=AF.Copy),
                           reads=[PS[bk]], writes=[B("evb", i)])
                    r0 = t * TT + sbk * 128
                    sch.dma(POOL, sc_dv[g, 4 * hb:4 * hb + 4, r0:r0 + 128, :].rearrange("h p j -> p h j"),
                            evb[i][:].rearrange("p (h j) -> p h j", j=128),
                            reads=[B("evb", i)], writes=[B("sc_dv", g, t, hb, sbk)])


    def dil_phase_b(layer):
        scale = 128 ** -0.5
        for h in range(16):
            first = True
            for g, d in enumerate(DIL):
                bi = (h * 3 + g) % 2
                sch.dma(SP, P[f"akn{bi}"][:], sc_dk[g, h, :, :], reads=[B("sc_dk", g, h, t) for t in range(NT)],
                        writes=[B("akn", bi)])
                sch.dma(SP, P[f"aqf{bi}"][:], sc_dq[g, h, :, :], reads=[B("sc_dq", g, h, t) for t in range(NT)],
                        writes=[B("aqfull", bi)])
                nblk = T // (128 * d)
                first_v = True
                for r in range(d):
                    srcv = sc_dv[g, h].rearrange("(b k dd) j -> dd k b j", k=128, dd=d)[r]
                    step = min(nblk, 8)
                    for b0 in range(0, nblk, step):
                        sch.dma(SP, P[f"akv{bi}"][:, r * nblk + b0:r * nblk + b0 + step, :], srcv[:, b0:b0 + step, :],
                                reads=[B("sc_dv", g, t, h // 4, s_) for t in range(NT) for s_ in range(4)] if first_v else [],
                                writes=[B("akv", bi)], add=not first_v)
                        first_v = False
                for r in range(d):
                    for b in range(nblk):
                        c0 = 128 * b * d + r
                        qcols = slice(c0, c0 + 127 * d + 1, d)
                        bk = next_bank(0, 4)
                        nprev = 1 if b > 0 else 0
                        if b > 0:
                            p0 = 128 * (b - 1) * d + r
                            pcols = slice(p0, p0 + 127 * d + 1, d)
                            sch.op(PE, lambda e: e.matmul(ps[:, bk, 0:128], lhsT=P[f"akn{bi}"][:, pcols], rhs=P[f"aqf{bi}"][:, qcols],
                                                          start=True, stop=True),
                                   reads=[B("akn", bi), B("aqfull", bi)], writes=[PS[bk]], sig=False)
                        sch.op(PE, lambda e: e.matmul(ps[:, bk, 128:256], lhsT=P[f"akn{bi}"][:, qcols], rhs=P[f"aqf{bi}"][:, qcols],
                                                      start=True, stop=True),
                               reads=[B("akn", bi), B("aqfull", bi)], writes=[PS[bk]], sig=True)
                        pi = state["ev"] % 3
                        state["ev"] += 1
                        lo = 0 if b > 0 else 128
                        sch.op(ACT, lambda e: e.activation(out=apt[pi][:, lo:256], in_=ps[:, bk, lo:256], func=AF.Exp, scale=scale),
                               reads=[PS[bk]], writes=[B("apt", pi)])
                        mk = cst["mask_d"]
                        sch.op(DVE, lambda e: e.tensor_tensor(
                            out=apt[pi][:, lo:256], in0=apt[pi][:, lo:256],
                            in1=mk[:].rearrange("p a q -> p (a q)")[:, lo:256], op=ALU.mult),
                            reads=[B("apt", pi)], writes=[B("apt", pi)])
                        ob = 4 + (state["bank2"] % 2)
                        db = 6 + (state["bank2"] % 2)
                        state["bank2"] += 1
                        n = r * nblk + b
                        if b > 0:
                            sch.op(PE, lambda e: e.matmul(ps[:, ob, 0:128], lhsT=P[f"akv{bi}"][:, n - 1, :], rhs=apt[pi][:, 0:128],
                                                          start=True, stop=False),
                                   reads=[B("akv", bi), B("apt", pi)], writes=[PS[ob]], sig=False)
                            sch.op(PE, lambda e: e.matmul(ps[:, db, 0:128], lhsT=cst["ones_b"][:], rhs=apt[pi][:, 0:128],
                                                          start=True, stop=False),
                                   reads=[B("apt", pi)], writes=[PS[db]], sig=False)
                        sch.op(PE, lambda e: e.matmul(ps[:, ob, 0:128], lhsT=P[f"akv{bi}"][:, n, :], rhs=apt[pi][:, 128:256],
                                                      start=(b == 0), stop=True),
                               reads=[B("akv", bi), B("apt", pi)], writes=[PS[ob]], sig=False)
                        sch.op(PE, lambda e: e.matmul(ps[:, db, 0:128], lhsT=cst["ones_b"][:], rhs=apt[pi][:, 128:256],
                                                      start=(b == 0), stop=True),
                               reads=[B("apt", pi)], writes=[PS[db]], sig=True)
                        if g == 0:
                            sch.op(ACT, lambda e: e.activation(out=P["oacc"][:, qcols], in_=ps[:, ob, 0:128], func=AF.Copy),
                                   reads=[PS[ob]], writes=[B("oacc")])
                            sch.op(ACT, lambda e: e.activation(out=P["dacc"][:, qcols], in_=ps[:, db, 0:128], func=AF.Copy),
                                   reads=[PS[db]], writes=[B("dacc")])
                        else:
                            sch.op(DVE, lambda e: e.tensor_tensor(out=P["oacc"][:, qcols], in0=ps[:, ob, 0:128], in1=P["oacc"][:, qcols], op=ALU.add),
                                   reads=[PS[ob], B("oacc")], writes=[B("oacc")])
                            sch.op(DVE, lambda e: e.tensor_tensor(out=P["dacc"][:, qcols], in0=ps[:, db, 0:128], in1=P["dacc"][:, qcols], op=ALU.add),
                                   reads=[PS[db], B("dacc")], writes=[B("dacc")])
            sch.op(DVE, lambda e: e.reciprocal(out=P["dacc"][:], in_=P["dacc"][:]), reads=[B("dacc")], writes=[B("dacc")])
            for qi in range(NT):
                qsl = slice(qi * TT, (qi + 1) * TT)
                i = next_evb()
                sch.op(DVE, lambda e: e.tensor_tensor(out=evb[i][:], in0=P["oacc"][:, qsl], in1=P["dacc"][:, qsl], op=ALU.mult),
                       reads=[B("oacc"), B("dacc")], writes=[B("evb", i)])
                sch.dma(POOL, sc_o[h, :, qsl], evb[i][:], reads=[B("evb", i)], writes=[B("sc_o", h, qi)])


    def run_phase(kind, fn, *args):
        with ExitStack() as pes:
            phase_alloc(pes, kind)
            fn(*args)
            barrier()

    for layer in range(depth):
        last = layer == depth - 1
        if layer % 2 == 0:
            run_phase("A", mla_phase_a, layer)
            run_phase("BM", mla_phase_b, layer)
        else:
            run_phase("A", dil_phase_a, layer)
            run_phase("BD", dil_phase_b, layer)
        run_phase("C", phase_c, layer, f"L{layer}_wo", last)

    for t in range(NT):
        b = sch.bufs.get(("y", t))
        if b is not None:
            for tk in b.w:
                sch._wait(SP, tk)
    for E in (PE, ACT, DVE, POOL):
        if E.count > 0:
            sch._wait(SP, (E.sem, E.count, E.name))
    for Q in (SP, POOL, ACT):
        for i, sem in enumerate(Q.dma_sems):
            if Q.dma_cnt[i] > 0:
                sch._wait(SP, (sem, Q.dma_cnt[i], f"{Q.name}.d{i}"))
    es.close()
    return nc


_PROGRAM_CACHE = {}


def kernel(x, positions, attn_norm, ffn_norm, final_norm,
           mla_wq_a, mla_q_norm, mla_wq_b, mla_wkv_a, mla_kv_norm, mla_wkv_b, mla_wo,
           dil_w_in, dil_wo, ffn_w_up, ffn_conv_w, ffn_conv_b, ffn_w_down):
    inp = dict(x=x, positions=positions, attn_norm=attn_norm, ffn_norm=ffn_norm, final_norm=final_norm,
               mla_wq_a=mla_wq_a, mla_q_norm=mla_q_norm, mla_wq_b=mla_wq_b, mla_wkv_a=mla_wkv_a,
               mla_kv_norm=mla_kv_norm, mla_wkv_b=mla_wkv_b, mla_wo=mla_wo, dil_w_in=dil_w_in, dil_wo=dil_wo,
               ffn_w_up=ffn_w_up, ffn_conv_w=ffn_conv_w, ffn_conv_b=ffn_conv_b, ffn_w_down=ffn_w_down)
    inp = {k: np.asarray(v) for k, v in inp.items()}
    out = run_model(inp, DEPTH, list(range(N_CORES)))
    return out


def common_inputs(inp, depth):
    shared = {}
    shared["g_attn"] = np.ascontiguousarray(inp["attn_norm"].reshape(DEPTH, DC, 128).transpose(2, 0, 1))
    shared["g_ffn"] = np.ascontiguousarray(inp["ffn_norm"].reshape(DEPTH, DC, 128).transpose(2, 0, 1))
    shared["g_fin"] = lay_vec(inp["final_norm"])
    shared["g_qn"] = np.ascontiguousarray(inp["mla_q_norm"].reshape(2, 4, 128).transpose(2, 0, 1))
    shared["g_kvn"] = np.ascontiguousarray(inp["mla_kv_norm"].reshape(2, 4, 128).transpose(2, 0, 1))
    shared["conv_w"] = np.ascontiguousarray(inp["ffn_conv_w"].reshape(DEPTH, 3, 88, 128).transpose(3, 0, 1, 2))
    shared["conv_b"] = np.ascontiguousarray(inp["ffn_conv_b"].reshape(DEPTH, 88, 128).transpose(2, 0, 1))
    shared.update(const_inputs())
    shared.update(prep_weights(inp, depth))
    return shared


def run_model(inp, depth, core_ids, **bkw):
    key = (depth, tuple(sorted(bkw.items())))
    if key not in _PROGRAM_CACHE:
        _PROGRAM_CACHE[key] = build_program(depth=depth, **bkw)
    nc = _PROGRAM_CACHE[key]
    shared = common_inputs(inp, depth)
    in_maps = []
    for c in core_ids:
        b = c % NB
        m = dict(shared)
        m["xT"] = np.ascontiguousarray(inp["x"][b].T.reshape(DC, 128, S))
        m["pos"] = np.ascontiguousarray(np.broadcast_to(inp["positions"][b].astype(np.int32)[None, :], (128, S)))
        in_maps.append(m)
    res = run_bass_kernel_spmd(nc, in_maps, core_ids=list(range(len(core_ids))))
    outs = []
    for i in range(min(NB, len(core_ids))):
        yT = res.results[i]["yT"].reshape(D, S)
        outs.append(np.ascontiguousarray(yT.T))
    return np.stack(outs, axis=0).astype(np.float32)
```

```python
import math
from contextlib import ExitStack

import numpy as np
import ml_dtypes

import concourse.bass as bass
import concourse.mybir as mybir
from concourse.bass_utils import run_bass_kernel_spmd

F32 = mybir.dt.float32
BF16 = mybir.dt.bfloat16
I32 = mybir.dt.int32
AF = mybir.ActivationFunctionType
ALU = mybir.AluOpType

D = 2048
DC = D // 128
S = 4096
DEPTH = 4
TT = 512
EPS = 1e-6
THETA = 500000.0
FH = 5632
FC = FH // 128
NB = 4
N_CORES = 8
TC = S // 2
NFLAG = 16


class Buf:
    __slots__ = ("w", "r", "name")

    def __init__(self, name=""):
        self.w = []
        self.r = {}
        self.name = name


class Eng:
    def __init__(self, name, eng, sem, dma_sems):
        self.name = name
        self.eng = eng
        self.sem = sem
        self.count = 0
        self.seen = {}
        self.dma_sems = dma_sems
        self.dma_cnt = [0] * len(dma_sems)
        self.dma_i = 0


class Sched:
    def __init__(self, nc, es):
        self.nc = nc
        self.bufs = {}

        def sems(prefix, n):
            return [es.enter_context(nc.semaphore(f"{prefix}{i}")) for i in range(n)]

        self.pe = Eng("pe", nc.tensor, sems("s_pe", 1)[0], [])
        self.act = Eng("act", nc.scalar, sems("s_act", 1)[0], sems("d_act", 4))
        self.dve = Eng("dve", nc.vector, sems("s_dve", 1)[0], [])
        self.pool = Eng("pool", nc.gpsimd, sems("s_pool", 1)[0], sems("d_pool", 16))
        self.sp = Eng("sp", nc.sync, sems("s_sp", 1)[0], sems("d_sp", 16))
        self.engs = {e.name: e for e in (self.pe, self.act, self.dve, self.pool, self.sp)}
        self.nwaits = 0

    def B(self, *key):
        b = self.bufs.get(key)
        if b is None:
            b = Buf(str(key))
            self.bufs[key] = b
        return b

    def _wait(self, E, tk):
        sem, val, key = tk
        if E.seen.get(key, 0) >= val:
            return
        if key == "pe" and E.name == "pe":
            return
        src = self.engs.get(key)
        if src is not None and val > src.count:
            raise RuntimeError(f"wait on unsignaled ticket {key}:{val} > {src.count}")
        E.eng.wait_ge(sem, val)
        E.seen[key] = val
        self.nwaits += 1

    def _deps(self, E, reads, writes, add):
        for b in reads:
            for tk in b.w:
                self._wait(E, tk)
        for b in writes:
            if not add:
                for tk in b.w:
                    self._wait(E, tk)
            for tk in b.r.values():
                self._wait(E, tk)

    def _post(self, tk, rkey, reads, writes, add):
        for b in reads:
            b.r[rkey] = tk
        for b in writes:
            if add:
                b.w.append(tk)
            else:
                b.w = [tk]
            b.r = {}

    def op(self, E, fn, reads=(), writes=(), sig=True, add=False):
        self._deps(E, reads, writes, add)
        ins = fn(E.eng)
        if sig:
            E.count += 1
            ins.then_inc(E.sem, 1)
            tk = (E.sem, E.count, E.name)
        else:
            tk = (E.sem, E.count + 1, E.name)
        self._post(tk, E.name, reads, writes, add)
        return ins

    def dma(self, Q, out, in_, reads=(), writes=(), add=False):
        self._deps(Q, reads, writes, add)
        ins = Q.eng.dma_start(out=out, in_=in_)
        i = Q.dma_i % len(Q.dma_sems)
        Q.dma_i += 1
        Q.dma_cnt[i] += 16
        ins.then_inc(Q.dma_sems[i], 16)
        key = f"{Q.name}.d{i}"
        tk = (Q.dma_sems[i], Q.dma_cnt[i], key)
        self._post(tk, key, reads, writes, add)
        return tk

    def dma_sel(self, Q, role_reg, out_fn, in_fn, reads=(), writes=(), add=False):
        self._deps(Q, reads, writes, add)
        i = Q.dma_i % len(Q.dma_sems)
        Q.dma_i += 1
        Q.dma_cnt[i] += 16
        with Q.eng.If_eq(role_reg, 0):
            Q.eng.dma_start(out=out_fn(0), in_=in_fn(0)).then_inc(Q.dma_sems[i], 16)
        with Q.eng.Else():
            Q.eng.dma_start(out=out_fn(1), in_=in_fn(1)).then_inc(Q.dma_sems[i], 16)
        key = f"{Q.name}.d{i}"
        tk = (Q.dma_sems[i], Q.dma_cnt[i], key)
        self._post(tk, key, reads, writes, add)
        return tk

    def wait_all(self, E, bufs):
        for b in bufs:
            for tk in b.w:
                self._wait(E, tk)


def lay_lhsT(W):
    K, N = W.shape
    return np.ascontiguousarray(W.reshape(K // 128, 128, N // 128, 128).transpose(2, 1, 0, 3))


def lay_rhs(W):
    K, N = W.shape
    return np.ascontiguousarray(W.reshape(K // 128, 128, N // 512, 512).transpose(2, 1, 0, 3))


def lay_vec(v):
    return np.ascontiguousarray(v.reshape(-1, 128).T)


def weight_specs(depth):
    specs = []
    for i in range(depth):
        j = i // 2
        if i % 2 == 0:
            specs += [
                (f"L{i}_wqa", [4, 128, 16, 128]),
                (f"L{i}_wqbn", [16, 128, 4, 128]),
                (f"L{i}_wqbp", [8, 128, 4, 128]),
                (f"L{i}_wkva", [4, 128, 16, 128]),
                (f"L{i}_wkpe", [1, 128, 16, 64]),
                (f"L{i}_wkbk", [16, 128, 4, 128]),
                (f"L{i}_wkbv", [4, 128, 4, 512]),
                (f"L{i}_wo", [16, 128, 16, 128]),
            ]
        else:
            specs += [
                (f"L{i}_wqk", [96, 128, 16, 128]),
                (f"L{i}_wv", [12, 128, 16, 512]),
                (f"L{i}_wo", [16, 128, 16, 128]),
            ]
        specs += [
            (f"L{i}_wup", [88, 128, 16, 128]),
            (f"L{i}_wdn", [16, 128, 44, 128]),
        ]
    return specs


def prep_weights(inp, depth):
    out = {}
    for i in range(depth):
        j = i // 2
        if i % 2 == 0:
            out[f"L{i}_wqa"] = lay_lhsT(inp["mla_wq_a"][j])
            wqb = inp["mla_wq_b"][j].reshape(512, 16, 192)
            out[f"L{i}_wqbn"] = lay_lhsT(np.ascontiguousarray(wqb[:, :, :128]).reshape(512, 2048))
            out[f"L{i}_wqbp"] = lay_lhsT(np.ascontiguousarray(wqb[:, :, 128:]).reshape(512, 1024))
            wkva = inp["mla_wkv_a"][j]
            out[f"L{i}_wkva"] = lay_lhsT(np.ascontiguousarray(wkva[:, :512]))
            out[f"L{i}_wkpe"] = np.ascontiguousarray(
                wkva[:, 512:].reshape(16, 128, 1, 64).transpose(2, 1, 0, 3))
            wkvb = inp["mla_wkv_b"][j].reshape(512, 16, 256)
            out[f"L{i}_wkbk"] = lay_lhsT(np.ascontiguousarray(wkvb[:, :, :128]).reshape(512, 2048))
            out[f"L{i}_wkbv"] = lay_rhs(np.ascontiguousarray(wkvb[:, :, 128:]).reshape(512, 2048))
            out[f"L{i}_wo"] = lay_lhsT(inp["mla_wo"][j])
        else:
            win = inp["dil_w_in"][j].reshape(2048, 3, 3, 16, 128)
            wqk = np.ascontiguousarray(win[:, :, 0:2]).reshape(2048, 3 * 2 * 16 * 128)
            out[f"L{i}_wqk"] = lay_lhsT(wqk)
            wv = np.ascontiguousarray(win[:, :, 2]).reshape(2048, 3 * 16 * 128)
            out[f"L{i}_wv"] = lay_rhs(wv)
            out[f"L{i}_wo"] = lay_lhsT(inp["dil_wo"][j])
        out[f"L{i}_wup"] = lay_lhsT(inp["ffn_w_up"][i])
        out[f"L{i}_wdn"] = lay_lhsT(inp["ffn_w_down"][i])
    return out


def const_inputs():
    c = {}
    c["ones_f"] = np.ones((128, 128), np.float32)
    c["ones_b"] = np.ones((128, 128), ml_dtypes.bfloat16)
    pm = np.zeros((128, 128), np.float32)
    for m in range(128):
        blk, r = divmod(m, 64)
        pm[blk * 64 + (r + 32) % 64, m] = 1.0
    c["perm_m"] = pm
    pd = np.zeros((128, 128), np.float32)
    for m in range(32):
        pd[(m + 16) % 32, m] = 1.0
    c["perm_d"] = pd
    c["perm_mb"] = pm.astype(ml_dtypes.bfloat16)
    c["perm_db"] = pd.astype(ml_dtypes.bfloat16)
    two_pi = 2.0 * math.pi * (1.0 - 2e-7)
    rm = np.zeros((128, 4), np.float32)
    for p in range(128):
        r = p % 64
        i = r % 32
        rm[p, 0] = (THETA ** (-(2.0 * i) / 64.0)) / (2.0 * math.pi)
        rm[p, 1] = -two_pi if r < 32 else two_pi
        rm[p, 2] = two_pi
    c["rope_m"] = rm
    rd = np.zeros((128, 4), np.float32)
    for p in range(128):
        if p < 32:
            i = p % 16
            rd[p, 0] = (THETA ** (-(2.0 * i) / 32.0)) / (2.0 * math.pi)
            rd[p, 1] = -two_pi if p < 16 else two_pi
        else:
            rd[p, 0] = 0.0
            rd[p, 1] = two_pi
        rd[p, 2] = two_pi
    c["rope_d"] = rd
    k = np.arange(128)[:, None]
    q = np.arange(512)[None, :]
    mm = np.stack([((128 * a + k) <= q) for a in range(4)], axis=1)
    c["mask_m"] = np.ascontiguousarray(mm).astype(ml_dtypes.bfloat16)
    kk = np.arange(128)[:, None]
    qq = np.arange(128)[None, :]
    md = np.stack([kk >= qq, kk <= qq], axis=1)
    c["mask_d"] = np.ascontiguousarray(md).astype(ml_dtypes.bfloat16)
    md0 = md.copy()
    md0[:, 0, :] = False
    c["mask_d0"] = np.ascontiguousarray(md0).astype(ml_dtypes.bfloat16)
    return c


CONST_SPECS = [
    ("ones_f", [128, 128], F32), ("ones_b", [128, 128], BF16),
    ("perm_m", [128, 128], F32), ("perm_d", [128, 128], F32),
    ("perm_mb", [128, 128], BF16), ("perm_db", [128, 128], BF16),
    ("rope_m", [128, 4], F32), ("rope_d", [128, 4], F32),
    ("mask_m", [128, 4, 512], BF16), ("mask_d", [128, 2, 128], BF16), ("mask_d0", [128, 2, 128], BF16),
]


def build_program(depth=DEPTH, final_norm=True):
    T = TC
    NT = T // TT
    TS = 2 * T
    NKB = TS // 128
    HB = T // 128
    nc = bass.Bass("TRN2", target_bir_lowering=False)
    es = ExitStack()
    dram_in = {}

    def din(name, shape, dt):
        dram_in[name] = nc.dram_tensor(name, shape, dt, kind="ExternalInput").ap()
        return dram_in[name]

    def dscr(name, shape, dt):
        return nc.dram_tensor(name, shape, dt, kind="Internal").ap()

    def dshr(name, shape, dt):
        return nc.dram_tensor(name, shape, dt, kind="Internal", addr_space="Shared").ap()

    x_in = din("xT", [DC, 128, T], F32)
    pos_in = din("pos", [128, T], I32)
    role_in = din("role", [1, 1], I32)
    tok_in = din("tok", [1, NFLAG], I32)
    tok2_in = din("tok2", [1, NFLAG], I32)
    hbias_in = din("hbias", [128, 2], F32)
    g_attn = din("g_attn", [128, DEPTH, DC], F32)
    g_ffn = din("g_ffn", [128, DEPTH, DC], F32)
    g_fin = din("g_fin", [128, DC], F32)
    g_qn = din("g_qn", [128, 2, 4], F32)
    g_kvn = din("g_kvn", [128, 2, 4], F32)
    cw_in = din("conv_w", [128, DEPTH, 3, 88], F32)
    cb_in = din("conv_b", [128, DEPTH, 88], F32)
    for name, shape, dt in CONST_SPECS:
        din(name, shape, dt)
    wspecs = weight_specs(depth)
    w32 = {}
    w16 = {}
    for name, shape in wspecs:
        w32[name] = din(name, shape, F32)
        w16[name] = dscr(name + "_bf", shape, BF16)
    y_out = nc.dram_tensor("yT", [DC, 128, T], F32, kind="ExternalOutput").ap()

    xs = dscr("xs", [DC, 128, T], F32)
    sc_h2 = dscr("sc_h2", [DC, 128, T], BF16)
    sc_qn = dscr("sc_qn", [16, 128, T], BF16)
    sc_qp = dscr("sc_qp", [16, 64, T], BF16)
    sc_o = dscr("sc_o", [16, 128, T], BF16)
    sc_dq = dscr("sc_dq", [3, 16, 128, T], BF16)
    sh_kn = dshr("sh_kn", [2, 16, 128, T], BF16)
    sh_kp = dshr("sh_kp", [2, 64, T], BF16)
    sh_v = dshr("sh_v", [32, 128, HB, 128], BF16)
    sh_dk = dshr("sh_dk", [2, 48, 128, T], BF16)
    sh_dv = dshr("sh_dv", [96, T, 128], BF16)
    sh_h2 = dshr("sh_h2", [2, DC, 128, 2], BF16)
    sh_flag = dshr("sh_flag", [2, NFLAG], I32)
    sh_done = dshr("sh_done", [2, NFLAG], I32)

    sch = Sched(nc, es)
    state = {"bank": 0, "w": 0, "ev": 0, "evb": 0, "phase": 0, "flag": 0}
    PE, ACT, DVE, POOL, SP = sch.pe, sch.act, sch.dve, sch.pool, sch.sp
    B = sch.B

    def sb(name, shape, dt):
        return es.enter_context(nc.sbuf_tensor(name, shape, dt))

    role_rp = es.enter_context(nc.gpsimd.register("role_p"))
    role_rs = es.enter_context(nc.sync.register("role_s"))
    exp_r = es.enter_context(nc.sync.register("exp_r"))
    flg_r = es.enter_context(nc.sync.register("flg_r"))
    cnd_r = es.enter_context(nc.sync.register("cnd_r"))
    pexp_r = es.enter_context(nc.gpsimd.register("pexp_r"))
    pflg_r = es.enter_context(nc.gpsimd.register("pflg_r"))
    pcnd_r = es.enter_context(nc.gpsimd.register("pcnd_r"))
    nc.gpsimd.load(role_rp, role_in[0:1, 0:1])
    nc.sync.load(role_rs, role_in[0:1, 0:1])

    def storeP(out_fn, in_ap, **kw):
        return sch.dma_sel(POOL, role_rp, out_fn, lambda s_: in_ap, **kw)

    def loadS(out_ap, in_fn, **kw):
        return sch.dma_sel(SP, role_rs, lambda s_: out_ap, in_fn, **kw)

    ht = sb("ht", [128, DC, TT], BF16)
    NWB = 6
    wst = [sb(f"wst{i}", [128, 2048], BF16) for i in range(NWB)]
    sqb = [sb(f"sqb{i}", [128, TT], F32) for i in range(2)]
    rsb = sb("rsb", [128, TT], F32)
    NEV = 4
    evf = [sb(f"evf{i}", [128, TT + 2], F32) for i in range(NEV)]
    evc = [sb(f"evc{i}", [128, TT], F32) for i in range(NEV)]
    roph = [sb(f"roph{i}", [128, TT], BF16) for i in range(NEV)]
    ropl = [sb(f"ropl{i}", [128, TT], BF16) for i in range(NEV)]
    NEB = 4
    evb = [sb(f"evb{i}", [128, TT], BF16) for i in range(NEB)]
    uhalo = sb("uhalo", [128, 88, 2], F32)
    cgate = [sb(f"cgate{i}", [128, TT], F32) for i in range(2)]
    NPT = 4
    apt = [sb(f"apt{i}", [128, TT], BF16) for i in range(NPT)]
    arec = sb("arec", [128, TT], F32)
    hh = sb("hh", [128, DC, 2], BF16)
    hh2 = sb("hh2", [128, DC, 2], BF16)
    s_hb = sb("s_hb", [128, 2], F32)
    P = {}

    def phase_alloc(pes, kind):
        def a(name, shape, dt):
            P[name] = pes.enter_context(nc.sbuf_tensor(name + "_" + kind + str(state["phase"]), shape, dt))
        state["phase"] += 1
        if kind == "A":
            a("xt", [128, DC, TT], F32)
            a("small", [128, 4, TT], F32)
            a("smallb", [128, 4, TT], BF16)
            a("posi", [128, TT], I32)
            a("rt_u", [128, TT], F32)
            a("rt_i", [128, TT], I32)
            a("rt_f", [128, TT], F32)
            a("rt_m", [128, TT], F32)
            a("tabC", [128, TT], F32)
            a("tabS", [128, TT], F32)
        elif kind == "C1":
            a("xt0", [128, DC, TT], F32)
            a("xt1", [128, DC, TT], F32)
            a("ot", [128, DC, TT], BF16)
        elif kind == "C2":
            a("xt", [128, DC, TT], F32)
            a("gt", [128, FC, TT], BF16)
        elif kind == "BM":
            for i in range(2):
                a(f"akn{i}", [128, TS], BF16)
                a(f"akv{i}", [128, NKB, 128], BF16)
                a(f"aqn{i}", [128, TT], BF16)
                a(f"aqp{i}", [128, TT], BF16)
            a("akp", [128, TS], BF16)
        elif kind == "BD":
            for i in range(3):
                a(f"akn{i}", [128, TS], BF16)
                a(f"akv{i}", [128, NKB, 128], BF16)
                a(f"aqf{i}", [128, T], BF16)
            a("oacc", [128, T], F32)
            a("dacc", [128, T], F32)

    def barrier():
        tks = []
        for E in (PE, ACT, DVE, POOL):
            if E.count > 0:
                tks.append((E.sem, E.count, E.name))
        for Q in (SP, POOL, ACT):
            for i, sem in enumerate(Q.dma_sems):
                if Q.dma_cnt[i] > 0:
                    tks.append((sem, Q.dma_cnt[i], f"{Q.name}.d{i}"))
        for E in (PE, ACT, DVE, POOL, SP):
            for tk in tks:
                sch._wait(E, tk)

    def pool_drain_dmas():
        for i, sem in enumerate(POOL.dma_sems):
            if POOL.dma_cnt[i] > 0:
                sch._wait(POOL, (sem, POOL.dma_cnt[i], f"pool.d{i}"))

    def post_flag():
        k = state["flag"]
        state["flag"] += 1
        pool_drain_dmas()
        storeP(lambda s_: sh_flag[s_:s_ + 1, k:k + 1], tok_in[0:1, k:k + 1])
        return k

    def post_done(k):
        storeP(lambda s_: sh_done[s_:s_ + 1, k:k + 1], tok2_in[0:1, k:k + 1])

    def poll_done(k):
        g = nc.gpsimd
        g.load(pexp_r, tok2_in[0:1, k:k + 1])
        with g.If_eq(role_rp, 0):
            g.reg_mov(pcnd_r, 1)
            with g.While(pcnd_r):
                g.load(pflg_r, sh_done[1:2, k:k + 1])
                g.reg_sub(pcnd_r, pexp_r, pflg_r)
        with g.Else():
            g.reg_mov(pcnd_r, 1)
            with g.While(pcnd_r):
                g.load(pflg_r, sh_done[0:1, k:k + 1])
                g.reg_sub(pcnd_r, pexp_r, pflg_r)

    def poll_flag(k):
        sp = nc.sync
        sp.load(exp_r, tok_in[0:1, k:k + 1])
        sp.reg_mov(cnd_r, 1)
        with sp.While(cnd_r):
            sp.load(flg_r, sh_flag[0:1, k:k + 1])
            sp.reg_sub(cnd_r, exp_r, flg_r)

    cst = {}
    for name, shape, dt in CONST_SPECS:
        cst[name] = sb("c_" + name, shape, dt)
    s_gattn = sb("s_gattn", [128, DEPTH, DC], F32)
    s_gffn = sb("s_gffn", [128, DEPTH, DC], F32)
    s_gfin = sb("s_gfin", [128, DC], F32)
    s_gqn = sb("s_gqn", [128, 2, 4], F32)
    s_gkvn = sb("s_gkvn", [128, 2, 4], F32)
    s_cw = sb("s_cw", [128, DEPTH, 3, 88], F32)
    s_cb = sb("s_cb", [128, DEPTH, 88], F32)
    ps = es.enter_context(nc.psum_tensor("ps", [128, 8, 512], F32))
    PS = [B("ps", i) for i in range(8)]

    def next_bank(lo=0, hi=8):
        b = state["bank"]
        if b < lo or b >= hi:
            b = lo
        state["bank"] = b + 1 if b + 1 < hi else lo
        return b

    for name, shape, dt in CONST_SPECS:
        sch.dma(SP, cst[name][:], dram_in[name], writes=[B("c", name)])
    for dst, src_, nm in ((s_gattn, g_attn, "ga"), (s_gffn, g_ffn, "gf"), (s_gfin, g_fin, "gn"),
                          (s_gqn, g_qn, "gq"), (s_gkvn, g_kvn, "gk"), (s_cw, cw_in, "cw"), (s_cb, cb_in, "cb"),
                          (s_hb, hbias_in, "hb")):
        sch.dma(SP, dst[:], src_, writes=[B("c", nm)])
    CONSTS = [B("c", n) for n, _, _ in CONST_SPECS] + [B("c", n) for n in ("ga", "gf", "gn", "gq", "gk", "cw", "cb", "hb")]
    for E in (PE, ACT, DVE, POOL):
        sch.wait_all(E, CONSTS)

    wgrp = {}
    cast_q = []
    cast_done = set()
    for name, shape in wspecs:
        n0 = shape[0]
        per = shape[1] * shape[2] * shape[3]
        grp = max(1, (1 << 18) // per)
        wgrp[name] = grp
        lay = int(name[1:name.index("_")])
        for o0 in range(0, n0, grp):
            cast_q.append((name, o0, min(n0, o0 + grp), o0 // grp, lay))
    cast_pos = {"i": 0, "tick": 0, "layer": 0, "vt": 0.0, "next": 0.0}

    def pace(dt_us):
        cast_pos["vt"] += dt_us
        iv = 14.0 if cast_pos["layer"] == 0 else 24.0
        while cast_pos["vt"] >= cast_pos["next"]:
            pump_casts(1)
            cast_pos["next"] += iv

    def pump_casts(n, force=False):
        while n > 0 and cast_pos["i"] < len(cast_q):
            name, o0, o1, gi, lay = cast_q[cast_pos["i"]]
            if not force and lay > cast_pos["layer"] + 1:
                return
            cast_pos["i"] += 1
            sch.dma(POOL, w16[name][o0:o1], w32[name][o0:o1], writes=[B("w", name, gi)])
            cast_done.add((name, gi))
            n -= 1

    def wbuf(name, oc):
        gi = oc // wgrp[name]
        while (name, gi) not in cast_done:
            pump_casts(1, force=True)
        return B("w", name, gi)

    def load_stage(name, oc, k0, kn, width=128):
        i = state["w"] % NWB
        state["w"] += 1
        view = wst[i][:, 0:kn * width].rearrange("p (k w) -> p k w", w=width)
        wb_ = wbuf(name, oc)
        sch.dma(SP, view, w16[name][oc, :, k0:k0 + kn, :], reads=[wb_], writes=[B("wst", i)])
        pace(0.22 * kn * (width // 128))
        return view, B("wst", i)

    def proj_fm(name, n_oc, KC, rhs_fn, rhs_bufs, evac, width=128, extra=None):
        pending = None
        for oc in range(n_oc):
            bk = next_bank()
            k0 = 0
            while k0 < KC:
                kn = min(16, KC - k0)
                wv, wb = load_stage(name, oc, k0, kn, width)
                if extra is not None:
                    extra(oc, wv, wb, k0, kn)
                for kk in range(kn):
                    kc = k0 + kk
                    last = kc == KC - 1
                    sch.op(PE, lambda e, kk=kk, kc=kc, last=last: e.matmul(
                        ps[0:width, bk, :], lhsT=wv[:, kk, :], rhs=rhs_fn(kc), start=(kc == 0), stop=last),
                        reads=[wb] + rhs_bufs, writes=[PS[bk]], sig=last)
                k0 += kn
            if pending is not None:
                pending()
            pending = evac(oc, ps[0:width, bk, :], PS[bk])
        if pending is not None:
            pending()

    eps_t = sb("eps_t", [128, 1], F32)
    sch.op(DVE, lambda e: e.memset(eps_t[:], EPS), writes=[B("eps")])
    for E in (ACT, POOL, PE):
        sch.wait_all(E, [B("eps")])

    def rmsnorm(src_fn, src_bufs, nch, dim, gain_ap_fn, dst_fn, dst_bufs):
        bk = next_bank()
        for c in range(nch):
            q = sqb[c % 2]
            sch.op(ACT, lambda e, c=c, q=q: e.activation(out=q[:], in_=src_fn(c), func=AF.Square),
                   reads=src_bufs, writes=[B("sqb", c % 2)])
            sch.op(PE, lambda e, c=c, q=q: e.matmul(ps[:, bk, :], lhsT=cst["ones_f"][:], rhs=q[:],
                                                    start=(c == 0), stop=(c == nch - 1)),
                   reads=[B("sqb", c % 2)], writes=[PS[bk]], sig=True)
        sch.op(ACT, lambda e: e.activation(out=rsb[:], in_=ps[:, bk, :], func=AF.Sqrt, bias=eps_t[:], scale=1.0 / dim),
               reads=[PS[bk]], writes=[B("rsb")])
        sch.op(DVE, lambda e: e.reciprocal(out=rsb[:], in_=rsb[:]), reads=[B("rsb")], writes=[B("rsb")])
        for c in range(nch):
            sch.op(DVE, lambda e, c=c: e.scalar_tensor_tensor(
                out=dst_fn(c), in0=src_fn(c), scalar=gain_ap_fn(c), in1=rsb[:], op0=ALU.mult, op1=ALU.mult),
                reads=src_bufs + [B("rsb")], writes=dst_bufs)

    def rope_tables(t, kind):
        rc = cst["rope_m"] if kind == "m" else cst["rope_d"]
        rt_u, rt_i, rt_f, rt_m = P["rt_u"], P["rt_i"], P["rt_f"], P["rt_m"]
        sch.dma(SP, P["posi"][:], pos_in[:, t * TT:(t + 1) * TT], writes=[B("posi")])
        sch.op(DVE, lambda e: e.tensor_copy(out=rt_f[:], in_=P["posi"][:]), reads=[B("posi")], writes=[B("rt_f")])
        for which in (0, 1):
            off = 0.25 if which == 0 else 0.0
            dst = P["tabC"] if which == 0 else P["tabS"]
            dstb = B("tabC") if which == 0 else B("tabS")
            sch.op(DVE, lambda e: e.tensor_scalar(out=rt_u[:], in0=rt_f[:], scalar1=rc[:, 0:1], scalar2=off,
                                                  op0=ALU.mult, op1=ALU.add),
                   reads=[B("rt_f")], writes=[B("rt_u")])
            sch.op(DVE, lambda e: e.tensor_copy(out=rt_i[:], in_=rt_u[:]), reads=[B("rt_u")], writes=[B("rt_i")])
            sch.op(DVE, lambda e: e.tensor_copy(out=rt_m[:], in_=rt_i[:]), reads=[B("rt_i")], writes=[B("rt_m")])
            sch.op(DVE, lambda e: e.tensor_tensor(out=rt_u[:], in0=rt_u[:], in1=rt_m[:], op=ALU.subtract),
                   reads=[B("rt_u"), B("rt_m")], writes=[B("rt_u")])
            sch.op(DVE, lambda e: e.tensor_single_scalar(out=rt_m[:], in_=rt_u[:], scalar=0.5, op=ALU.is_gt),
                   reads=[B("rt_u")], writes=[B("rt_m")])
            sch.op(DVE, lambda e: e.tensor_tensor(out=rt_u[:], in0=rt_u[:], in1=rt_m[:], op=ALU.subtract),
                   reads=[B("rt_u"), B("rt_m")], writes=[B("rt_u")])
            sch.op(DVE, lambda e: e.tensor_single_scalar(out=rt_m[:], in_=rt_u[:], scalar=-0.5, op=ALU.is_lt),
                   reads=[B("rt_u")], writes=[B("rt_m")])
            sch.op(DVE, lambda e: e.tensor_tensor(out=rt_u[:], in0=rt_u[:], in1=rt_m[:], op=ALU.add),
                   reads=[B("rt_u"), B("rt_m")], writes=[B("rt_u")])
            sc = rc[:, 2:3] if which == 0 else rc[:, 1:2]
            sch.op(ACT, lambda e, dst=dst, sc=sc: e.activation(out=dst[:], in_=rt_u[:], func=AF.Sin, scale=sc),
                   reads=[B("rt_u")], writes=[dstb])

    def rope_apply(src_ps, src_buf, nparts, perm, out_bf, out_buf, after):
        i = state["ev"] % NEV
        state["ev"] += 1
        xh = roph[i]
        xl = ropl[i]
        xf = evf[i]
        xc = evc[i]
        sch.op(ACT, lambda e: e.activation(out=xh[0:nparts, :], in_=src_ps, func=AF.Copy),
               reads=[src_buf], writes=[B("roph", i)])
        sch.op(DVE, lambda e: e.tensor_tensor(out=xl[0:nparts, :], in0=src_ps, in1=xh[0:nparts, :], op=ALU.subtract),
               reads=[src_buf, B("roph", i)], writes=[B("ropl", i)])
        sch.op(DVE, lambda e: e.tensor_tensor(out=xc[0:nparts, :], in0=src_ps, in1=P["tabC"][0:nparts, :], op=ALU.mult),
               reads=[src_buf, B("tabC")], writes=[B("evc", i)])

        def stage2():
            bk = next_bank()
            sch.op(PE, lambda e: e.matmul(ps[0:nparts, bk, :], lhsT=cst["perm_mb" if perm is cst["perm_m"] else "perm_db"][0:nparts, 0:nparts],
                                          rhs=xh[0:nparts, :], start=True, stop=False),
                   reads=[B("roph", i)], writes=[PS[bk]], sig=False)
            sch.op(PE, lambda e: e.matmul(ps[0:nparts, bk, :], lhsT=cst["perm_mb" if perm is cst["perm_m"] else "perm_db"][0:nparts, 0:nparts],
                                          rhs=xl[0:nparts, :], start=False, stop=True),
                   reads=[B("ropl", i)], writes=[PS[bk]], sig=True)
            sch.op(DVE, lambda e: e.tensor_tensor(out=xf[0:nparts, 0:TT], in0=ps[0:nparts, bk, :], in1=P["tabS"][0:nparts, :], op=ALU.mult),
                   reads=[PS[bk], B("tabS")], writes=[B("evf", i)])
            sch.op(DVE, lambda e: e.tensor_tensor(out=out_bf, in0=xc[0:nparts, :], in1=xf[0:nparts, 0:TT], op=ALU.add),
                   reads=[B("evf", i), B("evc", i)], writes=[out_buf])
            after()
        return stage2

    def next_evb():
        i = state["evb"] % NEB
        state["evb"] += 1
        return i

    def load_x(layer, t, from_xs=False):
        src_ = xs if (layer > 0 or from_xs) else x_in
        sch.dma(SP, P["xt"][:], src_[:, :, t * TT:(t + 1) * TT].rearrange("c p t -> p c t"),
                reads=[B("xs", t)], writes=[B("xt")])

    def evac_to(dst_ap, dst_buf, nparts=128, shared=False):
        def f(oc, pap, pbuf):
            i = next_evb()
            sch.op(ACT, lambda e: e.activation(out=evb[i][0:nparts, :], in_=pap, func=AF.Copy),
                   reads=[pbuf], writes=[B("evb", i)])
            if shared:
                storeP(lambda s_: dst_ap(oc, s_), evb[i][0:nparts, :], reads=[B("evb", i)], writes=[dst_buf(oc)])
            else:
                sch.dma(POOL, dst_ap(oc), evb[i][0:nparts, :], reads=[B("evb", i)], writes=[dst_buf(oc)])
            return None
        return f

    def v_proj(name, n_cb, KC, lhs_fn, lhs_bufs, dst_fn, dst_buf_fn):
        for cb in range(n_cb):
            banks = [next_bank() for _ in range(4)]
            k0 = 0
            while k0 < KC:
                kn = min(4, KC - k0)
                wv, wb = load_stage(name, cb, k0, kn, width=512)
                for sbk in range(4):
                    for kk in range(kn):
                        kc = k0 + kk
                        last = kc == KC - 1
                        sch.op(PE, lambda e, kk=kk, kc=kc, sbk=sbk, last=last: e.matmul(
                            ps[:, banks[sbk], :], lhsT=lhs_fn(kc, sbk), rhs=wv[:, kk, :], start=(kc == 0), stop=last),
                            reads=[wb] + lhs_bufs, writes=[PS[banks[sbk]]], sig=(kk == kn - 1))
                k0 += kn
            for sbk in range(4):
                i = next_evb()
                bk = banks[sbk]
                sch.op(ACT, lambda e, i=i, bk=bk: e.activation(out=evb[i][:], in_=ps[:, bk, :], func=AF.Copy),
                       reads=[PS[bk]], writes=[B("evb", i)])
                storeP(lambda s_, cb=cb, sbk=sbk: dst_fn(cb, sbk, s_), evb[i][:].rearrange("p (h j) -> p h j", j=128),
                       reads=[B("evb", i)], writes=[dst_buf_fn(cb, sbk)])

    def mla_phase_a(layer):
        j = layer // 2
        xt, small, smallb = P["xt"], P["small"], P["smallb"]
        for t in range(NT):
            tsl = slice(t * TT, (t + 1) * TT)
            load_x(layer, t)
            rmsnorm(lambda c: xt[:, c, :], [B("xt")], DC, D, lambda c: s_gattn[:, layer, c:c + 1],
                    lambda c: ht[:, c, :], [B("ht")])
            rope_tables(t, "m")

            def ev_small(oc, pap, pbuf):
                sch.op(ACT, lambda e: e.activation(out=small[:, oc, :], in_=pap, func=AF.Copy),
                       reads=[pbuf], writes=[B("small")])
            proj_fm(f"L{layer}_wqa", 4, 16, lambda kc: ht[:, kc, :], [B("ht")], ev_small)
            rmsnorm(lambda c: small[:, c, :], [B("small")], 4, 512, lambda c: s_gqn[:, j, c:c + 1],
                    lambda c: smallb[:, c, :], [B("smallb")])
            proj_fm(f"L{layer}_wqbn", 16, 4, lambda kc: smallb[:, kc, :], [B("smallb")],
                    evac_to(lambda oc: sc_qn[oc, :, tsl], lambda oc: B("sc_qn", oc, t)))

            def ev_qpe(oc, pap, pbuf):
                i = next_evb()

                def after():
                    for hh_ in range(2):
                        sch.dma(POOL, sc_qp[2 * oc + hh_, :, tsl], evb[i][64 * hh_:64 * hh_ + 64, :],
                                reads=[B("evb", i)], writes=[B("sc_qp", 2 * oc + hh_, t)])
                return rope_apply(pap, pbuf, 128, cst["perm_m"], evb[i][:], B("evb", i), after)
            proj_fm(f"L{layer}_wqbp", 8, 4, lambda kc: smallb[:, kc, :], [B("smallb")], ev_qpe)
            proj_fm(f"L{layer}_wkva", 4, 16, lambda kc: ht[:, kc, :], [B("ht")], ev_small)

            def ev_kpe(oc, pap, pbuf):
                i = next_evb()

                def after():
                    storeP(lambda s_: sh_kp[s_, :, tsl], evb[i][0:64, :],
                           reads=[B("evb", i)], writes=[B("sh_kp", t)])
                return rope_apply(pap, pbuf, 64, cst["perm_m"], evb[i][0:64, :], B("evb", i), after)
            proj_fm(f"L{layer}_wkpe", 1, 16, lambda kc: ht[:, kc, :], [B("ht")], ev_kpe, width=64)
            rmsnorm(lambda c: small[:, c, :], [B("small")], 4, 512, lambda c: s_gkvn[:, j, c:c + 1],
                    lambda c: smallb[:, c, :], [B("smallb")])
            proj_fm(f"L{layer}_wkbk", 16, 4, lambda kc: smallb[:, kc, :], [B("smallb")],
                    evac_to(lambda oc, s_: sh_kn[s_, oc, :, tsl], lambda oc: B("sh_kn", oc, t), shared=True))
            v_proj(f"L{layer}_wkbv", 4, 4, lambda kc, sbk: smallb[:, kc, sbk * 128:(sbk + 1) * 128], [B("smallb")],
                   lambda cb, sbk, s_: sh_v[s_ * 16 + 4 * cb:s_ * 16 + 4 * cb + 4, :, t * 4 + sbk, :].rearrange("h p j -> p h j"),
                   lambda cb, sbk: B("sh_v", t, cb, sbk))

    def mla_phase_b(layer, flag_k):
        scale = (128 + 64) ** -0.5
        poll_flag(flag_k)
        akp = P["akp"]
        sch.op(POOL, lambda e: e.memset(akp[64:128, :], 0.0), writes=[B("akpz")])
        for i_ in range(2):
            sch.op(POOL, lambda e, i_=i_: e.memset(P[f"aqp{i_}"][64:128, :], 0.0), writes=[B("aqpz", i_)])
        sch.dma(SP, akp[0:64, 0:T], sh_kp[0], writes=[B("akp")])
        loadS(akp[0:64, T:TS], lambda s_: sh_kp[s_],
              reads=[B("sh_kp", t) for t in range(NT)], writes=[B("akp")], add=True)
        its = []
        for h in range(16):
            for qi in range(NT):
                nkb = HB + 4 * qi + 4
                for kb in range(nkb):
                    its.append((h, qi, kb, nkb))

        def load_head(h):
            bi = h % 2
            akn, akv = P[f"akn{bi}"], P[f"akv{bi}"]
            sch.dma(SP, akn[:, 0:T], sh_kn[0, h], writes=[B("akn", bi)])
            loadS(akn[:, T:TS], lambda s_: sh_kn[s_, h],
                  reads=[B("sh_kn", h, t) for t in range(NT)], writes=[B("akn", bi)], add=True)
            sch.dma(SP, akv[:, 0:HB, :], sh_v[h], writes=[B("akv", bi)])
            loadS(akv[:, HB:NKB, :], lambda s_: sh_v[s_ * 16 + h],
                  reads=[B("sh_v", t, h // 4, x_) for t in range(NT) for x_ in range(4)],
                  writes=[B("akv", bi)], add=True)

        def load_q(h, qi):
            qb = (h * NT + qi) % 2
            qsl = slice(qi * TT, (qi + 1) * TT)
            sch.dma(SP, P[f"aqn{qb}"][:], sc_qn[h, :, qsl], reads=[B("sc_qn", h, qi)], writes=[B("aqn", qb)])
            sch.dma(SP, P[f"aqp{qb}"][0:64, :], sc_qp[h, :, qsl], reads=[B("sc_qp", h, qi)], writes=[B("aqp", qb)])

        def s1(it):
            h, qi, kb, nkb = it
            bi = h % 2
            qb = (h * NT + qi) % 2
            akn, akv = P[f"akn{bi}"], P[f"akv{bi}"]
            aqn, aqp = P[f"aqn{qb}"], P[f"aqp{qb}"]
            if h == 0 and qi == 0 and kb == 0:
                load_head(0)
                load_q(0, 0)
            if kb == 0:
                nq = h * NT + qi + 1
                if nq < 16 * NT:
                    load_q(nq // NT, nq % NT)
            if qi == 0 and kb == 3 and h + 1 < 16:
                load_head(h + 1)
            bk = next_bank(0, 4)
            ksl = slice(kb * 128, (kb + 1) * 128)
            pace(0.9)
            sch.op(PE, lambda e: e.matmul(ps[:, bk, :], lhsT=akn[:, ksl], rhs=aqn[:], start=True, stop=False),
                   reads=[B("akn", bi), B("aqn", qb)], writes=[PS[bk]], sig=False)
            sch.op(PE, lambda e: e.matmul(ps[:, bk, :], lhsT=akp[:, ksl], rhs=aqp[:], start=False, stop=True),
                   reads=[B("akp"), B("akpz"), B("aqp", qb), B("aqpz", qb)], writes=[PS[bk]], sig=True)
            pi = state["ev"] % NPT
            state["ev"] += 1
            if kb < HB:
                sch.op(ACT, lambda e: e.activation(out=apt[pi][:], in_=ps[:, bk, :], func=AF.Exp,
                                                   bias=s_hb[:, 0:1], scale=scale),
                       reads=[PS[bk]], writes=[B("apt", pi)])
            else:
                sch.op(ACT, lambda e: e.activation(out=apt[pi][:], in_=ps[:, bk, :], func=AF.Exp, scale=scale),
                       reads=[PS[bk]], writes=[B("apt", pi)])
            a = kb - HB - 4 * qi
            if a >= 0:
                sch.op(DVE, lambda e: e.tensor_tensor(out=apt[pi][:], in0=apt[pi][:], in1=cst["mask_m"][:, a, :],
                                                      op=ALU.mult),
                       reads=[B("apt", pi)], writes=[B("apt", pi)])
            return pi

        def s2(it, pi):
            h, qi, kb, nkb = it
            bi = h % 2
            akv = P[f"akv{bi}"]
            par = (h * NT + qi) % 2
            ob, db = 4 + par, 6 + par
            last = kb == nkb - 1
            sch.op(PE, lambda e: e.matmul(ps[:, ob, :], lhsT=akv[:, kb, :], rhs=apt[pi][:], start=(kb == 0), stop=last),
                   reads=[B("akv", bi), B("apt", pi)], writes=[PS[ob]], sig=False)
            sch.op(PE, lambda e: e.matmul(ps[:, db, :], lhsT=cst["ones_b"][:], rhs=apt[pi][:], start=(kb == 0), stop=last),
                   reads=[B("apt", pi)], writes=[PS[db]], sig=True)
            if last:
                qsl = slice(qi * TT, (qi + 1) * TT)
                sch.op(DVE, lambda e: e.reciprocal(out=arec[:], in_=ps[:, db, :]), reads=[PS[db]], writes=[B("arec")])
                i = next_evb()
                sch.op(DVE, lambda e: e.tensor_tensor(out=evb[i][:], in0=ps[:, ob, :], in1=arec[:], op=ALU.mult),
                       reads=[PS[ob], B("arec")], writes=[B("evb", i)])
                sch.dma(POOL, sc_o[h, :, qsl], evb[i][:], reads=[B("evb", i)], writes=[B("sc_o", h, qi)])

        LA = 2
        pis = {}
        n = len(its)
        for i in range(min(LA, n)):
            pis[i] = s1(its[i])
        for i in range(n):
            if i + LA < n:
                pis[i + LA] = s1(its[i + LA])
            s2(its[i], pis.pop(i))

    def phase_c1(layer, wo_name):
        ot = P["ot"]
        src_ = xs if layer > 0 else x_in

        def loads(t):
            tsl_ = slice(t * TT, (t + 1) * TT)
            sch.dma(SP, ot[:], sc_o[:, :, tsl_].rearrange("c p t -> p c t"),
                    reads=[B("sc_o", h, t) for h in range(16)], writes=[B("ot")])
            sch.dma(SP, P[f"xt{t % 2}"][:], src_[:, :, tsl_].rearrange("c p t -> p c t"),
                    reads=[B("xs", t)], writes=[B("xt", t % 2)])

        loads(0)
        for t in range(NT):
            tsl = slice(t * TT, (t + 1) * TT)
            xt = P[f"xt{t % 2}"]
            xb = B("xt", t % 2)

            def ev_res(oc, pap, pbuf, xt=xt, xb=xb):
                sch.op(DVE, lambda e: e.tensor_tensor(out=xt[:, oc, :], in0=pap, in1=xt[:, oc, :], op=ALU.add),
                       reads=[pbuf, xb], writes=[xb])
            proj_fm(wo_name, 16, 16, lambda kc: ot[:, kc, :], [B("ot")], ev_res)
            if t + 1 < NT:
                loads(t + 1)
            rmsnorm(lambda c: xt[:, c, :], [xb], DC, D, lambda c: s_gffn[:, layer, c:c + 1],
                    lambda c: ht[:, c, :], [B("ht")])
            sch.dma(POOL, xs[:, :, tsl].rearrange("c p t -> p c t"), xt[:], reads=[xb], writes=[B("xs", t)])
            sch.dma(POOL, sc_h2[:, :, tsl].rearrange("c p t -> p c t"), ht[:], reads=[B("ht")], writes=[B("sc_h2", t)])
            if t == NT - 1:
                storeP(lambda s_: sh_h2[s_].rearrange("c p t -> p c t"), ht[:, :, TT - 2:TT],
                       reads=[B("ht")], writes=[B("sh_h2")])

    def phase_c2(layer, last_layer, flag_k):
        xt, gt = P["xt"], P["gt"]
        name = f"L{layer}_wup"
        poll_flag(flag_k)
        sch.dma(SP, hh[:], sh_h2[0].rearrange("c p t -> p c t"), writes=[B("hh")])
        sch.op(DVE, lambda e: e.tensor_scalar(out=hh2[:], in0=hh[:], scalar1=s_hb[:, 1:2], scalar2=None, op0=ALU.mult),
               reads=[B("hh")], writes=[B("hh2")])
        for t in range(NT):
            tsl = slice(t * TT, (t + 1) * TT)
            load_x(layer, t, from_xs=True)
            sch.dma(SP, ht[:], sc_h2[:, :, tsl].rearrange("c p t -> p c t"), reads=[B("sc_h2", t)], writes=[B("ht")])

            def conv_chunk(oc, pap, pbuf, dst, dst_buf):
                i = state["ev"] % NEV
                state["ev"] += 1
                u = evf[i]
                ub = B("evf", i)
                sch.op(POOL, lambda e: e.tensor_copy(out=u[:, 0:2], in_=uhalo[:, oc, :]),
                       reads=[B("uhalo", oc)], writes=[ub])
                sch.op(ACT, lambda e: e.activation(out=u[:, 2:TT + 2], in_=pap, func=AF.Copy),
                       reads=[pbuf], writes=[ub])
                sch.op(POOL, lambda e: e.tensor_copy(out=uhalo[:, oc, :], in_=u[:, TT:TT + 2]),
                       reads=[ub], writes=[B("uhalo", oc)])
                sch.op(ACT, lambda e: e.activation(out=dst, in_=u[:, 2:TT + 2], func=AF.Identity,
                                                   bias=s_cb[:, layer, oc:oc + 1], scale=s_cw[:, layer, 2, oc:oc + 1]),
                       reads=[ub], writes=[dst_buf])
                sch.op(DVE, lambda e: e.scalar_tensor_tensor(out=dst, in0=u[:, 1:TT + 1], scalar=s_cw[:, layer, 1, oc:oc + 1],
                                                             in1=dst, op0=ALU.mult, op1=ALU.add),
                       reads=[ub, dst_buf], writes=[dst_buf])
                sch.op(DVE, lambda e: e.scalar_tensor_tensor(out=dst, in0=u[:, 0:TT], scalar=s_cw[:, layer, 0, oc:oc + 1],
                                                             in1=dst, op0=ALU.mult, op1=ALU.add),
                       reads=[ub, dst_buf], writes=[dst_buf])

            def gate_finish(jc, res):
                ci = jc % 2
                cg = cgate[ci]
                conv_chunk(jc, ps[:, res["g"], :], PS[res["g"]], cg[:], B("cgate", ci))
                sch.op(ACT, lambda e: e.activation(out=cg[:], in_=cg[:], func=AF.Silu),
                       reads=[B("cgate", ci)], writes=[B("cgate", ci)])
                i = state["ev"] % NEV
                vb = evc[i]
                conv_chunk(FC + jc, ps[:, res["v"], :], PS[res["v"]], vb[:], B("evc", i))
                sch.op(DVE, lambda e: e.tensor_tensor(out=gt[:, jc, :], in0=cg[:], in1=vb[:], op=ALU.mult),
                       reads=[B("cgate", ci), B("evc", i)], writes=[B("gt")])

            pend = None
            for jc in range(FC):
                res = {}
                for which, oc in (("g", jc), ("v", FC + jc)):
                    bk = next_bank()
                    wv, wb = load_stage(name, oc, 0, 16)
                    if t == 0:
                        hbk = next_bank()
                        for kc in range(16):
                            sch.op(PE, lambda e, kc=kc: e.matmul(ps[:, hbk, 0:2], lhsT=wv[:, kc, :], rhs=hh2[:, kc, :],
                                                                 start=(kc == 0), stop=(kc == 15)),
                                   reads=[wb, B("hh2")], writes=[PS[hbk]], sig=(kc == 15))
                        sch.op(ACT, lambda e, oc=oc: e.activation(out=uhalo[:, oc, :], in_=ps[:, hbk, 0:2], func=AF.Copy),
                               reads=[PS[hbk]], writes=[B("uhalo", oc)])
                    for kc in range(16):
                        sch.op(PE, lambda e, kc=kc: e.matmul(ps[:, bk, :], lhsT=wv[:, kc, :], rhs=ht[:, kc, :],
                                                             start=(kc == 0), stop=(kc == 15)),
                               reads=[wb, B("ht")], writes=[PS[bk]], sig=(kc == 15))
                    res[which] = bk
                if pend is not None:
                    gate_finish(*pend)
                pend = (jc, res)
            gate_finish(*pend)

            def ev_res2(oc, pap, pbuf):
                sch.op(DVE, lambda e: e.tensor_tensor(out=xt[:, oc, :], in0=pap, in1=xt[:, oc, :], op=ALU.add),
                       reads=[pbuf, B("xt")], writes=[B("xt")])
            proj_fm(f"L{layer}_wdn", 16, FC, lambda kc: gt[:, kc, :], [B("gt")], ev_res2)
            if last_layer and final_norm:
                bk = next_bank()
                for c in range(DC):
                    q = sqb[c % 2]
                    sch.op(ACT, lambda e, c=c, q=q: e.activation(out=q[:], in_=xt[:, c, :], func=AF.Square),
                           reads=[B("xt")], writes=[B("sqb", c % 2)])
                    sch.op(PE, lambda e, c=c, q=q: e.matmul(ps[:, bk, :], lhsT=cst["ones_f"][:], rhs=q[:],
                                                            start=(c == 0), stop=(c == DC - 1)),
                           reads=[B("sqb", c % 2)], writes=[PS[bk]], sig=True)
                sch.op(ACT, lambda e: e.activation(out=rsb[:], in_=ps[:, bk, :], func=AF.Sqrt, bias=eps_t[:], scale=1.0 / D),
                       reads=[PS[bk]], writes=[B("rsb")])
                sch.op(DVE, lambda e: e.reciprocal(out=rsb[:], in_=rsb[:]), reads=[B("rsb")], writes=[B("rsb")])
                for c in range(DC):
                    sch.op(DVE, lambda e, c=c: e.scalar_tensor_tensor(
                        out=xt[:, c, :], in0=xt[:, c, :], scalar=s_gfin[:, c:c + 1], in1=rsb[:],
                        op0=ALU.mult, op1=ALU.mult), reads=[B("xt"), B("rsb")], writes=[B("xt")])
            if last_layer:
                sch.dma(SP, y_out[:, :, tsl].rearrange("c p t -> p c t"), xt[:], reads=[B("xt")], writes=[B("y", t)])
            else:
                sch.dma(SP, xs[:, :, tsl].rearrange("c p t -> p c t"), xt[:], reads=[B("xt")], writes=[B("xs", t)])

    DIL = (1, 4, 16)

    def dil_phase_a(layer):
        xt = P["xt"]
        for t in range(NT):
            tsl = slice(t * TT, (t + 1) * TT)
            load_x(layer, t)
            rmsnorm(lambda c: xt[:, c, :], [B("xt")], DC, D, lambda c: s_gattn[:, layer, c:c + 1],
                    lambda c: ht[:, c, :], [B("ht")])
            rope_tables(t, "d")

            def ev_qk(oc, pap, pbuf):
                g, rem = divmod(oc, 32)
                qk, h = divmod(rem, 16)
                i = next_evb()

                def after():
                    if qk == 0:
                        sch.dma(POOL, sc_dq[g, h, :, tsl], evb[i][:], reads=[B("evb", i)], writes=[B("sc_dq", g, h, t)])
                    else:
                        storeP(lambda s_: sh_dk[s_, g * 16 + h, :, tsl], evb[i][:],
                               reads=[B("evb", i)], writes=[B("sh_dk", g, h, t)])
                return rope_apply(pap, pbuf, 128, cst["perm_d"], evb[i][:], B("evb", i), after)
            proj_fm(f"L{layer}_wqk", 96, 16, lambda kc: ht[:, kc, :], [B("ht")], ev_qk)
            v_proj(f"L{layer}_wv", 12, 16, lambda kc, sbk: ht[:, kc, sbk * 128:(sbk + 1) * 128], [B("ht")],
                   lambda cb, sbk, s_: sh_dv[s_ * 48 + (cb // 4) * 16 + 4 * (cb % 4):s_ * 48 + (cb // 4) * 16 + 4 * (cb % 4) + 4,
                                             t * TT + sbk * 128:t * TT + sbk * 128 + 128, :].rearrange("h p j -> p h j"),
                   lambda cb, sbk: B("sh_dv", cb // 4, t, cb % 4, sbk))

    def dil_phase_b(layer, flag_k):
        scale = 128 ** -0.5
        poll_flag(flag_k)
        oacc, dacc = P["oacc"], P["dacc"]
        its = []
        for h in range(16):
            for g, d in enumerate(DIL):
                nbt = TS // (128 * d)
                for r in range(d):
                    for b in range(nbt // 2, nbt):
                        its.append((h, g, d, r, b, nbt))
        last_of_head = {}
        for idx, it in enumerate(its):
            last_of_head[it[0]] = idx

        def load_group(h, g):
            d = DIL[g]
            nbt = TS // (128 * d)
            bi = (h * 3 + g) % 3
            akn, akv, aqf = P[f"akn{bi}"], P[f"akv{bi}"], P[f"aqf{bi}"]
            gh = g * 16 + h
            sch.dma(SP, akn[:, 0:T], sh_dk[0, gh], writes=[B("akn", bi)])
            loadS(akn[:, T:TS], lambda s_: sh_dk[s_, gh],
                  reads=[B("sh_dk", g, h, t) for t in range(NT)], writes=[B("akn", bi)], add=True)
            sch.dma(SP, aqf[:], sc_dq[g, h], reads=[B("sc_dq", g, h, t) for t in range(NT)], writes=[B("aqf", bi)])
            hb_n = nbt // 2
            vreads = [B("sh_dv", g, t, h // 4, s_) for t in range(NT) for s_ in range(4)]
            for rr in range(d):
                srch = sh_dv[gh].rearrange("(b k dd) j -> dd k b j", k=128, dd=d)[rr]
                sch.dma(SP, akv[:, rr * nbt + hb_n - 1:rr * nbt + hb_n, :], srch[:, hb_n - 1:hb_n, :],
                        writes=[B("akv", bi)], add=not (rr == 0))
            for rr in range(d):
                step = min(hb_n, 8)
                for b0 in range(0, hb_n, step):
                    loadS(akv[:, rr * nbt + hb_n + b0:rr * nbt + hb_n + b0 + step, :],
                          lambda s_, rr=rr, b0=b0: sh_dv[s_ * 48 + gh].rearrange(
                              "(b k dd) j -> dd k b j", k=128, dd=d)[rr][:, b0:b0 + step, :],
                          reads=vreads if (rr == 0 and b0 == 0) else [], writes=[B("akv", bi)], add=True)

        it_local = {}
        _cnt = {}
        for it_ in its:
            k_ = (it_[0], it_[1])
            it_local[id(it_)] = _cnt.get(k_, 0)
            _cnt[k_] = _cnt.get(k_, 0) + 1

        def s1(it):
            h, g, d, r, b, nbt = it
            bi = (h * 3 + g) % 3
            akn, akv, aqf = P[f"akn{bi}"], P[f"akv{bi}"], P[f"aqf{bi}"]
            gi_ = h * 3 + g
            if gi_ == 0 and r == 0 and b == nbt // 2:
                load_group(0, 0)
                load_group(0, 1)
            if it_local[id(it)] == 4 and gi_ + 2 < 48:
                load_group((gi_ + 2) // 3, (gi_ + 2) % 3)
            c0 = 128 * b * d + r
            kcols = slice(c0, c0 + 127 * d + 1, d)
            pc0 = c0 - 128 * d
            pcols = slice(pc0, pc0 + 127 * d + 1, d)
            qcols = slice(c0 - T, c0 - T + 127 * d + 1, d)
            bk = next_bank(0, 4)
            pace(1.4)
            sch.op(PE, lambda e: e.matmul(ps[:, bk, 0:128], lhsT=akn[:, pcols], rhs=aqf[:, qcols], start=True, stop=True),
                   reads=[B("akn", bi), B("aqf", bi)], writes=[PS[bk]], sig=False)
            sch.op(PE, lambda e: e.matmul(ps[:, bk, 128:256], lhsT=akn[:, kcols], rhs=aqf[:, qcols], start=True, stop=True),
                   reads=[B("akn", bi), B("aqf", bi)], writes=[PS[bk]], sig=True)
            pi = state["ev"] % NPT
            state["ev"] += 1
            if b == nbt // 2:
                sch.op(ACT, lambda e: e.activation(out=apt[pi][:, 0:128], in_=ps[:, bk, 0:128], func=AF.Exp,
                                                   bias=s_hb[:, 0:1], scale=scale),
                       reads=[PS[bk]], writes=[B("apt", pi)])
                sch.op(ACT, lambda e: e.activation(out=apt[pi][:, 128:256], in_=ps[:, bk, 128:256], func=AF.Exp, scale=scale),
                       reads=[PS[bk]], writes=[B("apt", pi)], add=True)
            else:
                sch.op(ACT, lambda e: e.activation(out=apt[pi][:, 0:256], in_=ps[:, bk, 0:256], func=AF.Exp, scale=scale),
                       reads=[PS[bk]], writes=[B("apt", pi)])
            sch.op(DVE, lambda e: e.tensor_tensor(out=apt[pi][:, 0:256], in0=apt[pi][:, 0:256],
                                                  in1=cst["mask_d"][:].rearrange("p a q -> p (a q)"), op=ALU.mult),
                   reads=[B("apt", pi)], writes=[B("apt", pi)])
            return pi

        cnt = {"i": 0}

        def s2(it, pi, idx):
            h, g, d, r, b, nbt = it
            bi = (h * 3 + g) % 3
            akv = P[f"akv{bi}"]
            par = cnt["i"] % 2
            cnt["i"] += 1
            ob, db = 4 + par, 6 + par
            n = r * nbt + b
            c0 = 128 * b * d + r
            qcols = slice(c0 - T, c0 - T + 127 * d + 1, d)
            sch.op(PE, lambda e: e.matmul(ps[:, ob, 0:128], lhsT=akv[:, n - 1, :], rhs=apt[pi][:, 0:128], start=True, stop=False),
                   reads=[B("akv", bi), B("apt", pi)], writes=[PS[ob]], sig=False)
            sch.op(PE, lambda e: e.matmul(ps[:, ob, 0:128], lhsT=akv[:, n, :], rhs=apt[pi][:, 128:256], start=False, stop=True),
                   reads=[B("akv", bi), B("apt", pi)], writes=[PS[ob]], sig=False)
            sch.op(PE, lambda e: e.matmul(ps[:, db, 0:128], lhsT=cst["ones_b"][:], rhs=apt[pi][:, 0:128], start=True, stop=False),
                   reads=[B("apt", pi)], writes=[PS[db]], sig=False)
            sch.op(PE, lambda e: e.matmul(ps[:, db, 0:128], lhsT=cst["ones_b"][:], rhs=apt[pi][:, 128:256], start=False, stop=True),
                   reads=[B("apt", pi)], writes=[PS[db]], sig=True)
            if g == 0:
                regs = [(b - nbt // 2) // 4]
            elif g == 1:
                regs = [b - nbt // 2]
            else:
                regs = list(range(NT))
            ob_ = [B("oacc", q_) for q_ in regs]
            db_ = [B("dacc", q_) for q_ in regs]
            if g == 0:
                sch.op(ACT, lambda e: e.activation(out=oacc[:, qcols], in_=ps[:, ob, 0:128], func=AF.Copy),
                       reads=[PS[ob]], writes=ob_)
                sch.op(ACT, lambda e: e.activation(out=dacc[:, qcols], in_=ps[:, db, 0:128], func=AF.Copy),
                       reads=[PS[db]], writes=db_)
            else:
                sch.op(DVE, lambda e: e.tensor_tensor(out=oacc[:, qcols], in0=ps[:, ob, 0:128], in1=oacc[:, qcols], op=ALU.add),
                       reads=[PS[ob]] + ob_, writes=ob_)
                sch.op(DVE, lambda e: e.tensor_tensor(out=dacc[:, qcols], in0=ps[:, db, 0:128], in1=dacc[:, qcols], op=ALU.add),
                       reads=[PS[db]] + db_, writes=db_)
            if last_of_head[h] == idx:
                for qi in range(NT):
                    qsl = slice(qi * TT, (qi + 1) * TT)
                    sch.op(DVE, lambda e: e.reciprocal(out=dacc[:, qsl], in_=dacc[:, qsl]),
                           reads=[B("dacc", qi)], writes=[B("dacc", qi)])
                    i = next_evb()
                    sch.op(DVE, lambda e: e.tensor_tensor(out=evb[i][:], in0=oacc[:, qsl], in1=dacc[:, qsl], op=ALU.mult),
                           reads=[B("oacc", qi), B("dacc", qi)], writes=[B("evb", i)])
                    sch.dma(POOL, sc_o[h, :, qsl], evb[i][:], reads=[B("evb", i)], writes=[B("sc_o", h, qi)])

        LA = 3
        pis = {}
        n = len(its)
        for i in range(min(LA, n)):
            pis[i] = s1(its[i])
        for i in range(n):
            if i + LA < n:
                pis[i + LA] = s1(its[i + LA])
            s2(its[i], pis.pop(i), i)

    def run_phase(kind, tag, fn, *args):
        with ExitStack() as pes:
            phase_alloc(pes, kind)
            with nc.named_scope(tag):
                fn(*args)
                barrier()

    pump_casts(24)
    for layer in range(depth):
        last = layer == depth - 1
        cast_pos["layer"] = layer
        if layer % 2 == 0:
            run_phase("A", f"L{layer}_A", mla_phase_a, layer)
            k = post_flag()
            run_phase("BM", f"L{layer}_B", mla_phase_b, layer, k)
        else:
            run_phase("A", f"L{layer}_A", dil_phase_a, layer)
            k = post_flag()
            run_phase("BD", f"L{layer}_B", dil_phase_b, layer, k)
        if layer > 0:
            poll_done(layer - 1)
        run_phase("C1", f"L{layer}_C1", phase_c1, layer, f"L{layer}_wo")
        k = post_flag()
        run_phase("C2", f"L{layer}_C2", phase_c2, layer, last, k)
        if not last:
            post_done(layer)

    for t in range(NT):
        b = sch.bufs.get(("y", t))
        if b is not None:
            for tk in b.w:
                sch._wait(SP, tk)
    for E in (PE, ACT, DVE, POOL):
        if E.count > 0:
            sch._wait(SP, (E.sem, E.count, E.name))
    for Q in (SP, POOL, ACT):
        for i, sem in enumerate(Q.dma_sems):
            if Q.dma_cnt[i] > 0:
                sch._wait(SP, (sem, Q.dma_cnt[i], f"{Q.name}.d{i}"))
    es.close()
    return nc


_PROGRAM_CACHE = {}
_LAST = {}


def kernel(x, positions, attn_norm, ffn_norm, final_norm,
           mla_wq_a, mla_q_norm, mla_wq_b, mla_wkv_a, mla_kv_norm, mla_wkv_b, mla_wo,
           dil_w_in, dil_wo, ffn_w_up, ffn_conv_w, ffn_conv_b, ffn_w_down):
    inp = dict(x=x, positions=positions, attn_norm=attn_norm, ffn_norm=ffn_norm, final_norm=final_norm,
               mla_wq_a=mla_wq_a, mla_q_norm=mla_q_norm, mla_wq_b=mla_wq_b, mla_wkv_a=mla_wkv_a,
               mla_kv_norm=mla_kv_norm, mla_wkv_b=mla_wkv_b, mla_wo=mla_wo, dil_w_in=dil_w_in, dil_wo=dil_wo,
               ffn_w_up=ffn_w_up, ffn_conv_w=ffn_conv_w, ffn_conv_b=ffn_conv_b, ffn_w_down=ffn_w_down)
    inp = {k: np.asarray(v) for k, v in inp.items()}
    return run_model(inp, DEPTH, N_CORES)


def common_inputs(inp, depth):
    shared = {}
    shared["g_attn"] = np.ascontiguousarray(inp["attn_norm"].reshape(DEPTH, DC, 128).transpose(2, 0, 1))
    shared["g_ffn"] = np.ascontiguousarray(inp["ffn_norm"].reshape(DEPTH, DC, 128).transpose(2, 0, 1))
    shared["g_fin"] = lay_vec(inp["final_norm"])
    shared["g_qn"] = np.ascontiguousarray(inp["mla_q_norm"].reshape(2, 4, 128).transpose(2, 0, 1))
    shared["g_kvn"] = np.ascontiguousarray(inp["mla_kv_norm"].reshape(2, 4, 128).transpose(2, 0, 1))
    shared["conv_w"] = np.ascontiguousarray(inp["ffn_conv_w"].reshape(DEPTH, 3, 88, 128).transpose(3, 0, 1, 2))
    shared["conv_b"] = np.ascontiguousarray(inp["ffn_conv_b"].reshape(DEPTH, 88, 128).transpose(2, 0, 1))
    shared.update(const_inputs())
    shared.update(prep_weights(inp, depth))
    return shared


def run_model(inp, depth, n_cores, trace=False, **bkw):
    key = (depth, tuple(sorted(bkw.items())))
    if key not in _PROGRAM_CACHE:
        _PROGRAM_CACHE[key] = build_program(depth=depth, **bkw)
    nc = _PROGRAM_CACHE[key]
    shared = common_inputs(inp, depth)
    nonce = int(np.random.default_rng().integers(1, 2 ** 30))
    tok = (nonce + np.arange(NFLAG)).astype(np.int32)[None, :]
    tok2 = (nonce + 1000 + np.arange(NFLAG)).astype(np.int32)[None, :]
    in_maps = []
    for c in range(n_cores):
        b, role = divmod(c, 2)
        sl = slice(role * TC, (role + 1) * TC)
        m = dict(shared)
        m["xT"] = np.ascontiguousarray(inp["x"][b, sl].T.reshape(DC, 128, TC))
        m["pos"] = np.ascontiguousarray(np.broadcast_to(inp["positions"][b, sl].astype(np.int32)[None, :], (128, TC)))
        m["role"] = np.array([[role]], np.int32)
        m["tok"] = tok
        m["tok2"] = tok2
        hb = np.zeros((128, 2), np.float32)
        hb[:, 0] = -30000.0 if role == 0 else 0.0
        hb[:, 1] = 0.0 if role == 0 else 1.0
        m["hbias"] = hb
        in_maps.append(m)
    if trace:
        res = run_bass_kernel_spmd(nc, in_maps, core_ids=list(range(n_cores)), trace=True)
    else:
        res = run_bass_kernel_spmd(nc, in_maps, core_ids=list(range(n_cores)))
    _LAST["res"] = res
    outs = []
    for b in range(n_cores // 2):
        halves = [np.ascontiguousarray(res.results[2 * b + r]["yT"].reshape(D, TC).T) for r in range(2)]
        outs.append(np.concatenate(halves, axis=0))
    return np.stack(outs, axis=0).astype(np.float32)
```

```python
import math
from contextlib import ExitStack

import numpy as np
import ml_dtypes

import concourse.bass as bass
import concourse.mybir as mybir
from concourse.bass_utils import run_bass_kernel_spmd

F32 = mybir.dt.float32
BF16 = mybir.dt.bfloat16
I32 = mybir.dt.int32
AF = mybir.ActivationFunctionType
ALU = mybir.AluOpType

D = 2048
DC = D // 128
S = 4096
DEPTH = 4
TT = 512
EPS = 1e-6
THETA = 500000.0
FH = 5632
FC = FH // 128
NB = 4
N_CORES = 8
TC = S // 2
NFLAG = 16


class Buf:
    __slots__ = ("w", "r", "name")

    def __init__(self, name=""):
        self.w = []
        self.r = {}
        self.name = name


class Eng:
    def __init__(self, name, eng, sem, dma_sems):
        self.name = name
        self.eng = eng
        self.sem = sem
        self.count = 0
        self.seen = {}
        self.dma_sems = dma_sems
        self.dma_cnt = [0] * len(dma_sems)
        self.dma_i = 0


class Sched:
    def __init__(self, nc, es):
        self.nc = nc
        self.bufs = {}

        def sems(prefix, n):
            return [es.enter_context(nc.semaphore(f"{prefix}{i}")) for i in range(n)]

        self.pe = Eng("pe", nc.tensor, sems("s_pe", 1)[0], [])
        self.act = Eng("act", nc.scalar, sems("s_act", 1)[0], sems("d_act", 4))
        self.dve = Eng("dve", nc.vector, sems("s_dve", 1)[0], [])
        self.pool = Eng("pool", nc.gpsimd, sems("s_pool", 1)[0], sems("d_pool", 16))
        self.sp = Eng("sp", nc.sync, sems("s_sp", 1)[0], sems("d_sp", 16))
        self.engs = {e.name: e for e in (self.pe, self.act, self.dve, self.pool, self.sp)}
        self.nwaits = 0

    def B(self, *key):
        b = self.bufs.get(key)
        if b is None:
            b = Buf(str(key))
            self.bufs[key] = b
        return b

    def _wait(self, E, tk):
        sem, val, key = tk
        if E.seen.get(key, 0) >= val:
            return
        if key == "pe" and E.name == "pe":
            return
        src = self.engs.get(key)
        if src is not None and val > src.count:
            raise RuntimeError(f"wait on unsignaled ticket {key}:{val} > {src.count}")
        E.eng.wait_ge(sem, val)
        E.seen[key] = val
        self.nwaits += 1

    def _deps(self, E, reads, writes, add):
        for b in reads:
            for tk in b.w:
                self._wait(E, tk)
        for b in writes:
            if not add:
                for tk in b.w:
                    self._wait(E, tk)
            for tk in b.r.values():
                self._wait(E, tk)

    def _post(self, tk, rkey, reads, writes, add):
        for b in reads:
            b.r[rkey] = tk
        for b in writes:
            if add:
                b.w.append(tk)
            else:
                b.w = [tk]
            b.r = {}

    def op(self, E, fn, reads=(), writes=(), sig=True, add=False):
        self._deps(E, reads, writes, add)
        ins = fn(E.eng)
        if sig:
            E.count += 1
            ins.then_inc(E.sem, 1)
            tk = (E.sem, E.count, E.name)
        else:
            tk = (E.sem, E.count + 1, E.name)
        self._post(tk, E.name, reads, writes, add)
        return ins

    def dma(self, Q, out, in_, reads=(), writes=(), add=False):
        self._deps(Q, reads, writes, add)
        ins = Q.eng.dma_start(out=out, in_=in_)
        i = Q.dma_i % len(Q.dma_sems)
        Q.dma_i += 1
        Q.dma_cnt[i] += 16
        ins.then_inc(Q.dma_sems[i], 16)
        key = f"{Q.name}.d{i}"
        tk = (Q.dma_sems[i], Q.dma_cnt[i], key)
        self._post(tk, key, reads, writes, add)
        return tk

    def dma_sel(self, Q, role_reg, out_fn, in_fn, reads=(), writes=(), add=False):
        self._deps(Q, reads, writes, add)
        i = Q.dma_i % len(Q.dma_sems)
        Q.dma_i += 1
        Q.dma_cnt[i] += 16
        with Q.eng.If_eq(role_reg, 0):
            Q.eng.dma_start(out=out_fn(0), in_=in_fn(0)).then_inc(Q.dma_sems[i], 16)
        with Q.eng.Else():
            Q.eng.dma_start(out=out_fn(1), in_=in_fn(1)).then_inc(Q.dma_sems[i], 16)
        key = f"{Q.name}.d{i}"
        tk = (Q.dma_sems[i], Q.dma_cnt[i], key)
        self._post(tk, key, reads, writes, add)
        return tk

    def wait_all(self, E, bufs):
        for b in bufs:
            for tk in b.w:
                self._wait(E, tk)


def lay_lhsT(W):
    K, N = W.shape
    return np.ascontiguousarray(W.reshape(K // 128, 128, N // 128, 128).transpose(2, 1, 0, 3))


def lay_rhs(W):
    K, N = W.shape
    return np.ascontiguousarray(W.reshape(K // 128, 128, N // 512, 512).transpose(2, 1, 0, 3))


def lay_vec(v):
    return np.ascontiguousarray(v.reshape(-1, 128).T)


def weight_specs(depth):
    specs = []
    for i in range(depth):
        j = i // 2
        if i % 2 == 0:
            specs += [
                (f"L{i}_wqa", [4, 128, 16, 128]),
                (f"L{i}_wqbn", [16, 128, 4, 128]),
                (f"L{i}_wqbp", [8, 128, 4, 128]),
                (f"L{i}_wkva", [4, 128, 16, 128]),
                (f"L{i}_wkpe", [1, 128, 16, 64]),
                (f"L{i}_wkbk", [16, 128, 4, 128]),
                (f"L{i}_wkbv", [4, 128, 4, 512]),
                (f"L{i}_wo", [16, 128, 16, 128]),
            ]
        else:
            specs += [
                (f"L{i}_wqk", [96, 128, 16, 128]),
                (f"L{i}_wv", [12, 128, 16, 512]),
                (f"L{i}_wo", [16, 128, 16, 128]),
            ]
        specs += [
            (f"L{i}_wup", [88, 128, 16, 128]),
            (f"L{i}_wdn", [16, 128, 44, 128]),
        ]
    return specs


def prep_weights(inp, depth):
    out = {}
    for i in range(depth):
        j = i // 2
        if i % 2 == 0:
            out[f"L{i}_wqa"] = lay_lhsT(inp["mla_wq_a"][j])
            wqb = inp["mla_wq_b"][j].reshape(512, 16, 192)
            out[f"L{i}_wqbn"] = lay_lhsT(np.ascontiguousarray(wqb[:, :, :128]).reshape(512, 2048))
            out[f"L{i}_wqbp"] = lay_lhsT(np.ascontiguousarray(wqb[:, :, 128:]).reshape(512, 1024))
            wkva = inp["mla_wkv_a"][j]
            out[f"L{i}_wkva"] = lay_lhsT(np.ascontiguousarray(wkva[:, :512]))
            out[f"L{i}_wkpe"] = np.ascontiguousarray(
                wkva[:, 512:].reshape(16, 128, 1, 64).transpose(2, 1, 0, 3))
            wkvb = inp["mla_wkv_b"][j].reshape(512, 16, 256)
            out[f"L{i}_wkbk"] = lay_lhsT(np.ascontiguousarray(wkvb[:, :, :128]).reshape(512, 2048))
            out[f"L{i}_wkbv"] = lay_rhs(np.ascontiguousarray(wkvb[:, :, 128:]).reshape(512, 2048))
            out[f"L{i}_wo"] = lay_lhsT(inp["mla_wo"][j])
        else:
            win = inp["dil_w_in"][j].reshape(2048, 3, 3, 16, 128)
            wqk = np.ascontiguousarray(win[:, :, 0:2]).reshape(2048, 3 * 2 * 16 * 128)
            out[f"L{i}_wqk"] = lay_lhsT(wqk)
            wv = np.ascontiguousarray(win[:, :, 2]).reshape(2048, 3 * 16 * 128)
            out[f"L{i}_wv"] = lay_rhs(wv)
            out[f"L{i}_wo"] = lay_lhsT(inp["dil_wo"][j])
        out[f"L{i}_wup"] = lay_lhsT(inp["ffn_w_up"][i])
        out[f"L{i}_wdn"] = lay_lhsT(inp["ffn_w_down"][i])
    return out


def const_inputs():
    c = {}
    c["ones_f"] = np.ones((128, 128), np.float32)
    c["ones_b"] = np.ones((128, 128), ml_dtypes.bfloat16)
    pm = np.zeros((128, 128), np.float32)
    for m in range(128):
        blk, r = divmod(m, 64)
        pm[blk * 64 + (r + 32) % 64, m] = 1.0
    c["perm_m"] = pm
    pd = np.zeros((128, 128), np.float32)
    for m in range(32):
        pd[(m + 16) % 32, m] = 1.0
    c["perm_d"] = pd
    c["perm_mb"] = pm.astype(ml_dtypes.bfloat16)
    c["perm_db"] = pd.astype(ml_dtypes.bfloat16)
    two_pi = 2.0 * math.pi * (1.0 - 2e-7)
    rm = np.zeros((128, 4), np.float32)
    for p in range(128):
        r = p % 64
        i = r % 32
        rm[p, 0] = (THETA ** (-(2.0 * i) / 64.0)) / (2.0 * math.pi)
        rm[p, 1] = -two_pi if r < 32 else two_pi
        rm[p, 2] = two_pi
    c["rope_m"] = rm
    rd = np.zeros((128, 4), np.float32)
    for p in range(128):
        if p < 32:
            i = p % 16
            rd[p, 0] = (THETA ** (-(2.0 * i) / 32.0)) / (2.0 * math.pi)
            rd[p, 1] = -two_pi if p < 16 else two_pi
        else:
            rd[p, 0] = 0.0
            rd[p, 1] = two_pi
        rd[p, 2] = two_pi
    c["rope_d"] = rd
    k = np.arange(128)[:, None]
    q = np.arange(512)[None, :]
    mm = np.stack([((128 * a + k) <= q) for a in range(4)], axis=1)
    c["mask_m"] = np.ascontiguousarray(mm).astype(ml_dtypes.bfloat16)
    kk = np.arange(128)[:, None]
    qq = np.arange(128)[None, :]
    md = np.stack([kk >= qq, kk <= qq], axis=1)
    c["mask_d"] = np.ascontiguousarray(md).astype(ml_dtypes.bfloat16)
    md0 = md.copy()
    md0[:, 0, :] = False
    c["mask_d0"] = np.ascontiguousarray(md0).astype(ml_dtypes.bfloat16)
    return c


CONST_SPECS = [
    ("ones_f", [128, 128], F32), ("ones_b", [128, 128], BF16),
    ("perm_m", [128, 128], F32), ("perm_d", [128, 128], F32),
    ("perm_mb", [128, 128], BF16), ("perm_db", [128, 128], BF16),
    ("rope_m", [128, 4], F32), ("rope_d", [128, 4], F32),
    ("mask_m", [128, 4, 512], BF16), ("mask_d", [128, 2, 128], BF16), ("mask_d0", [128, 2, 128], BF16),
]


def build_program(depth=DEPTH, final_norm=True):
    T = TC
    NT = T // TT
    TS = 2 * T
    NKB = TS // 128
    HB = T // 128
    nc = bass.Bass("TRN2", target_bir_lowering=False)
    es = ExitStack()
    dram_in = {}

    def din(name, shape, dt):
        dram_in[name] = nc.dram_tensor(name, shape, dt, kind="ExternalInput").ap()
        return dram_in[name]

    def dscr(name, shape, dt):
        return nc.dram_tensor(name, shape, dt, kind="Internal").ap()

    def dshr(name, shape, dt):
        return nc.dram_tensor(name, shape, dt, kind="Internal", addr_space="Shared").ap()

    x_in = din("xT", [DC, 128, T], F32)
    pos_in = din("pos", [128, T], I32)
    role_in = din("role", [1, 1], I32)
    tok_in = din("tok", [1, NFLAG], I32)
    tok2_in = din("tok2", [1, NFLAG], I32)
    hbias_in = din("hbias", [128, 2], F32)
    g_attn = din("g_attn", [128, DEPTH, DC], F32)
    g_ffn = din("g_ffn", [128, DEPTH, DC], F32)
    g_fin = din("g_fin", [128, DC], F32)
    g_qn = din("g_qn", [128, 2, 4], F32)
    g_kvn = din("g_kvn", [128, 2, 4], F32)
    cw_in = din("conv_w", [128, DEPTH, 3, 88], F32)
    cb_in = din("conv_b", [128, DEPTH, 88], F32)
    for name, shape, dt in CONST_SPECS:
        din(name, shape, dt)
    wspecs = weight_specs(depth)
    w32 = {}
    w16 = {}
    for name, shape in wspecs:
        w32[name] = din(name, shape, F32)
        w16[name] = dscr(name + "_bf", shape, BF16)
    y_out = nc.dram_tensor("yT", [DC, 128, T], F32, kind="ExternalOutput").ap()

    xs = dscr("xs", [DC, 128, T], F32)
    sc_h2 = dscr("sc_h2", [DC, 128, T], BF16)
    sc_qn = dscr("sc_qn", [16, 128, T], BF16)
    sc_qp = dscr("sc_qp", [16, 64, T], BF16)
    sc_o = dscr("sc_o", [16, 128, T], BF16)
    sc_dq = dscr("sc_dq", [3, 16, 128, T], BF16)
    sh_kn = dshr("sh_kn", [2, 16, 128, T], BF16)
    sh_kp = dshr("sh_kp", [2, 64, T], BF16)
    sh_v = dshr("sh_v", [32, 128, HB, 128], BF16)
    sh_dk = dshr("sh_dk", [2, 48, 128, T], BF16)
    sh_dv = dshr("sh_dv", [96, T, 128], BF16)
    sh_h2 = dshr("sh_h2", [2, DC, 128, 2], BF16)
    sh_flag = dshr("sh_flag", [2, NFLAG], I32)
    sh_done = dshr("sh_done", [2, NFLAG], I32)
    gate_scr = dscr("gate_scr", [1, NFLAG], I32)

    sch = Sched(nc, es)
    state = {"bank": 0, "w": 0, "ev": 0, "evb": 0, "phase": 0, "flag": 0}
    PE, ACT, DVE, POOL, SP = sch.pe, sch.act, sch.dve, sch.pool, sch.sp
    B = sch.B

    def sb(name, shape, dt):
        return es.enter_context(nc.sbuf_tensor(name, shape, dt))

    role_rp = es.enter_context(nc.gpsimd.register("role_p"))
    role_rs = es.enter_context(nc.sync.register("role_s"))
    exp_r = es.enter_context(nc.sync.register("exp_r"))
    flg_r = es.enter_context(nc.sync.register("flg_r"))
    cnd_r = es.enter_context(nc.sync.register("cnd_r"))
    pexp_r = es.enter_context(nc.gpsimd.register("pexp_r"))
    pflg_r = es.enter_context(nc.gpsimd.register("pflg_r"))
    pcnd_r = es.enter_context(nc.gpsimd.register("pcnd_r"))
    nc.gpsimd.load(role_rp, role_in[0:1, 0:1])
    nc.sync.load(role_rs, role_in[0:1, 0:1])

    def storeP(out_fn, in_ap, **kw):
        return sch.dma_sel(POOL, role_rp, out_fn, lambda s_: in_ap, **kw)

    def loadS(out_ap, in_fn, **kw):
        return sch.dma_sel(SP, role_rs, lambda s_: out_ap, in_fn, **kw)

    ht = sb("ht", [128, DC, TT], BF16)
    NWB = 6
    wst = [sb(f"wst{i}", [128, 2048], BF16) for i in range(NWB)]
    sqb = [sb(f"sqb{i}", [128, TT], F32) for i in range(2)]
    rsb = sb("rsb", [128, TT], F32)
    NEV = 4
    evf = [sb(f"evf{i}", [128, TT + 2], F32) for i in range(NEV)]
    evc = [sb(f"evc{i}", [128, TT], F32) for i in range(NEV)]
    roph = [sb(f"roph{i}", [128, TT], BF16) for i in range(NEV)]
    ropl = [sb(f"ropl{i}", [128, TT], BF16) for i in range(NEV)]
    NEB = 4
    evb = [sb(f"evb{i}", [128, TT], BF16) for i in range(NEB)]
    uhalo = sb("uhalo", [128, 88, 2], F32)
    cgate = [sb(f"cgate{i}", [128, TT], F32) for i in range(2)]
    NPT = 4
    apt = [sb(f"apt{i}", [128, TT], BF16) for i in range(NPT)]
    arec = sb("arec", [128, TT], F32)
    hh = sb("hh", [128, DC, 2], BF16)
    hh2 = sb("hh2", [128, DC, 2], BF16)
    s_hb = sb("s_hb", [128, 2], F32)
    P = {}

    def phase_alloc(pes, kind):
        def a(name, shape, dt):
            P[name] = pes.enter_context(nc.sbuf_tensor(name + "_" + kind + str(state["phase"]), shape, dt))
        state["phase"] += 1
        if kind == "A":
            a("xt", [128, DC, TT], F32)
            a("small", [128, 4, TT], F32)
            a("smallb", [128, 4, TT], BF16)
            a("posi", [128, TT], I32)
            a("rt_u", [128, TT], F32)
            a("rt_i", [128, TT], I32)
            a("rt_f", [128, TT], F32)
            a("rt_m", [128, TT], F32)
            a("tabC", [128, TT], F32)
            a("tabS", [128, TT], F32)
        elif kind == "C1":
            a("xt0", [128, DC, TT], F32)
            a("xt1", [128, DC, TT], F32)
            a("ot", [128, DC, TT], BF16)
        elif kind == "C2":
            a("xt", [128, DC, TT], F32)
            a("gt", [128, FC, TT], BF16)
        elif kind == "BM":
            for i in range(2):
                a(f"akn{i}", [128, TS], BF16)
                a(f"akv{i}", [128, NKB, 128], BF16)
                a(f"aqn{i}", [128, TT], BF16)
                a(f"aqp{i}", [128, TT], BF16)
            a("akp", [128, TS], BF16)
        elif kind == "BD":
            for i in range(3):
                a(f"akn{i}", [128, TS], BF16)
                a(f"akv{i}", [128, NKB, 128], BF16)
                a(f"aqf{i}", [128, T], BF16)
            a("oacc", [128, T], F32)
            a("dacc", [128, T], F32)

    def barrier():
        tks = []
        for E in (PE, ACT, DVE, POOL):
            if E.count > 0:
                tks.append((E.sem, E.count, E.name))
        for Q in (SP, POOL, ACT):
            for i, sem in enumerate(Q.dma_sems):
                if Q.dma_cnt[i] > 0:
                    tks.append((sem, Q.dma_cnt[i], f"{Q.name}.d{i}"))
        for E in (PE, ACT, DVE, POOL, SP):
            for tk in tks:
                sch._wait(E, tk)

    def pool_drain_dmas():
        for i, sem in enumerate(POOL.dma_sems):
            if POOL.dma_cnt[i] > 0:
                sch._wait(POOL, (sem, POOL.dma_cnt[i], f"pool.d{i}"))

    def post_flag():
        k = state["flag"]
        state["flag"] += 1
        pool_drain_dmas()
        storeP(lambda s_: sh_flag[s_:s_ + 1, k:k + 1], tok_in[0:1, k:k + 1])
        return k

    def post_done(k):
        storeP(lambda s_: sh_done[s_:s_ + 1, k:k + 1], tok2_in[0:1, k:k + 1])

    def poll_done(k):
        sp = nc.sync
        sp.load(exp_r, tok2_in[0:1, k:k + 1])
        with sp.If_eq(role_rs, 0):
            sp.reg_mov(cnd_r, 1)
            with sp.While(cnd_r):
                sp.load(flg_r, sh_done[1:2, k:k + 1])
                sp.reg_sub(cnd_r, exp_r, flg_r)
        with sp.Else():
            sp.reg_mov(cnd_r, 1)
            with sp.While(cnd_r):
                sp.load(flg_r, sh_done[0:1, k:k + 1])
                sp.reg_sub(cnd_r, exp_r, flg_r)
        sch.dma(SP, gate_scr[0:1, k:k + 1], tok2_in[0:1, k:k + 1], writes=[B("revgate")])
        sch.wait_all(POOL, [B("revgate")])

    def poll_flag(k):
        sp = nc.sync
        sp.load(exp_r, tok_in[0:1, k:k + 1])
        sp.reg_mov(cnd_r, 1)
        with sp.While(cnd_r):
            sp.load(flg_r, sh_flag[0:1, k:k + 1])
            sp.reg_sub(cnd_r, exp_r, flg_r)

    cst = {}
    for name, shape, dt in CONST_SPECS:
        cst[name] = sb("c_" + name, shape, dt)
    s_gattn = sb("s_gattn", [128, DEPTH, DC], F32)
    s_gffn = sb("s_gffn", [128, DEPTH, DC], F32)
    s_gfin = sb("s_gfin", [128, DC], F32)
    s_gqn = sb("s_gqn", [128, 2, 4], F32)
    s_gkvn = sb("s_gkvn", [128, 2, 4], F32)
    s_cw = sb("s_cw", [128, DEPTH, 3, 88], F32)
    s_cb = sb("s_cb", [128, DEPTH, 88], F32)
    ps = es.enter_context(nc.psum_tensor("ps", [128, 8, 512], F32))
    PS = [B("ps", i) for i in range(8)]

    def next_bank(lo=0, hi=8):
        b = state["bank"]
        if b < lo or b >= hi:
            b = lo
        state["bank"] = b + 1 if b + 1 < hi else lo
        return b

    for name, shape, dt in CONST_SPECS:
        sch.dma(SP, cst[name][:], dram_in[name], writes=[B("c", name)])
    for dst, src_, nm in ((s_gattn, g_attn, "ga"), (s_gffn, g_ffn, "gf"), (s_gfin, g_fin, "gn"),
                          (s_gqn, g_qn, "gq"), (s_gkvn, g_kvn, "gk"), (s_cw, cw_in, "cw"), (s_cb, cb_in, "cb"),
                          (s_hb, hbias_in, "hb")):
        sch.dma(SP, dst[:], src_, writes=[B("c", nm)])
    CONSTS = [B("c", n) for n, _, _ in CONST_SPECS] + [B("c", n) for n in ("ga", "gf", "gn", "gq", "gk", "cw", "cb", "hb")]
    for E in (PE, ACT, DVE, POOL):
        sch.wait_all(E, CONSTS)

    wgrp = {}
    cast_q = []
    cast_done = set()
    for name, shape in wspecs:
        n0 = shape[0]
        per = shape[1] * shape[2] * shape[3]
        grp = max(1, (1 << 18) // per)
        wgrp[name] = grp
        lay = int(name[1:name.index("_")])
        for o0 in range(0, n0, grp):
            cast_q.append((name, o0, min(n0, o0 + grp), o0 // grp, lay))
    cast_pos = {"i": 0, "tick": 0, "layer": 0, "vt": 0.0, "next": 0.0}

    def pace(dt_us):
        cast_pos["vt"] += dt_us
        iv = 14.0 if cast_pos["layer"] == 0 else 24.0
        while cast_pos["vt"] >= cast_pos["next"]:
            pump_casts(1)
            cast_pos["next"] += iv

    def pump_casts(n, force=False):
        while n > 0 and cast_pos["i"] < len(cast_q):
            name, o0, o1, gi, lay = cast_q[cast_pos["i"]]
            if not force and lay > cast_pos["layer"] + 1:
                return
            cast_pos["i"] += 1
            sch.dma(POOL, w16[name][o0:o1], w32[name][o0:o1], writes=[B("w", name, gi)])
            cast_done.add((name, gi))
            n -= 1

    def wbuf(name, oc):
        gi = oc // wgrp[name]
        while (name, gi) not in cast_done:
            pump_casts(1, force=True)
        return B("w", name, gi)

    def load_stage(name, oc, k0, kn, width=128):
        i = state["w"] % NWB
        state["w"] += 1
        view = wst[i][:, 0:kn * width].rearrange("p (k w) -> p k w", w=width)
        wb_ = wbuf(name, oc)
        sch.dma(SP, view, w16[name][oc, :, k0:k0 + kn, :], reads=[wb_], writes=[B("wst", i)])
        pace(0.22 * kn * (width // 128))
        return view, B("wst", i)

    def proj_fm(name, n_oc, KC, rhs_fn, rhs_bufs, evac, width=128, extra=None):
        pending = None
        for oc in range(n_oc):
            bk = next_bank()
            k0 = 0
            while k0 < KC:
                kn = min(16, KC - k0)
                wv, wb = load_stage(name, oc, k0, kn, width)
                if extra is not None:
                    extra(oc, wv, wb, k0, kn)
                for kk in range(kn):
                    kc = k0 + kk
                    last = kc == KC - 1
                    sch.op(PE, lambda e, kk=kk, kc=kc, last=last: e.matmul(
                        ps[0:width, bk, :], lhsT=wv[:, kk, :], rhs=rhs_fn(kc), start=(kc == 0), stop=last),
                        reads=[wb] + rhs_bufs, writes=[PS[bk]], sig=last)
                k0 += kn
            if pending is not None:
                pending()
            pending = evac(oc, ps[0:width, bk, :], PS[bk])
        if pending is not None:
            pending()

    eps_t = sb("eps_t", [128, 1], F32)
    sch.op(DVE, lambda e: e.memset(eps_t[:], EPS), writes=[B("eps")])
    for E in (ACT, POOL, PE):
        sch.wait_all(E, [B("eps")])

    def rmsnorm(src_fn, src_bufs, nch, dim, gain_ap_fn, dst_fn, dst_bufs):
        bk = next_bank()
        for c in range(nch):
            q = sqb[c % 2]
            sch.op(ACT, lambda e, c=c, q=q: e.activation(out=q[:], in_=src_fn(c), func=AF.Square),
                   reads=src_bufs, writes=[B("sqb", c % 2)])
            sch.op(PE, lambda e, c=c, q=q: e.matmul(ps[:, bk, :], lhsT=cst["ones_f"][:], rhs=q[:],
                                                    start=(c == 0), stop=(c == nch - 1)),
                   reads=[B("sqb", c % 2)], writes=[PS[bk]], sig=True)
        sch.op(ACT, lambda e: e.activation(out=rsb[:], in_=ps[:, bk, :], func=AF.Sqrt, bias=eps_t[:], scale=1.0 / dim),
               reads=[PS[bk]], writes=[B("rsb")])
        sch.op(DVE, lambda e: e.reciprocal(out=rsb[:], in_=rsb[:]), reads=[B("rsb")], writes=[B("rsb")])
        for c in range(nch):
            sch.op(DVE, lambda e, c=c: e.scalar_tensor_tensor(
                out=dst_fn(c), in0=src_fn(c), scalar=gain_ap_fn(c), in1=rsb[:], op0=ALU.mult, op1=ALU.mult),
                reads=src_bufs + [B("rsb")], writes=dst_bufs)

    def rope_tables(t, kind):
        rc = cst["rope_m"] if kind == "m" else cst["rope_d"]
        rt_u, rt_i, rt_f, rt_m = P["rt_u"], P["rt_i"], P["rt_f"], P["rt_m"]
        sch.dma(SP, P["posi"][:], pos_in[:, t * TT:(t + 1) * TT], writes=[B("posi")])
        sch.op(DVE, lambda e: e.tensor_copy(out=rt_f[:], in_=P["posi"][:]), reads=[B("posi")], writes=[B("rt_f")])
        for which in (0, 1):
            off = 0.25 if which == 0 else 0.0
            dst = P["tabC"] if which == 0 else P["tabS"]
            dstb = B("tabC") if which == 0 else B("tabS")
            sch.op(DVE, lambda e: e.tensor_scalar(out=rt_u[:], in0=rt_f[:], scalar1=rc[:, 0:1], scalar2=off,
                                                  op0=ALU.mult, op1=ALU.add),
                   reads=[B("rt_f")], writes=[B("rt_u")])
            sch.op(DVE, lambda e: e.tensor_copy(out=rt_i[:], in_=rt_u[:]), reads=[B("rt_u")], writes=[B("rt_i")])
            sch.op(DVE, lambda e: e.tensor_copy(out=rt_m[:], in_=rt_i[:]), reads=[B("rt_i")], writes=[B("rt_m")])
            sch.op(DVE, lambda e: e.tensor_tensor(out=rt_u[:], in0=rt_u[:], in1=rt_m[:], op=ALU.subtract),
                   reads=[B("rt_u"), B("rt_m")], writes=[B("rt_u")])
            sch.op(DVE, lambda e: e.tensor_single_scalar(out=rt_m[:], in_=rt_u[:], scalar=0.5, op=ALU.is_gt),
                   reads=[B("rt_u")], writes=[B("rt_m")])
            sch.op(DVE, lambda e: e.tensor_tensor(out=rt_u[:], in0=rt_u[:], in1=rt_m[:], op=ALU.subtract),
                   reads=[B("rt_u"), B("rt_m")], writes=[B("rt_u")])
            sch.op(DVE, lambda e: e.tensor_single_scalar(out=rt_m[:], in_=rt_u[:], scalar=-0.5, op=ALU.is_lt),
                   reads=[B("rt_u")], writes=[B("rt_m")])
            sch.op(DVE, lambda e: e.tensor_tensor(out=rt_u[:], in0=rt_u[:], in1=rt_m[:], op=ALU.add),
                   reads=[B("rt_u"), B("rt_m")], writes=[B("rt_u")])
            sc = rc[:, 2:3] if which == 0 else rc[:, 1:2]
            sch.op(ACT, lambda e, dst=dst, sc=sc: e.activation(out=dst[:], in_=rt_u[:], func=AF.Sin, scale=sc),
                   reads=[B("rt_u")], writes=[dstb])

    def rope_apply(src_ps, src_buf, nparts, perm, out_bf, out_buf, after):
        i = state["ev"] % NEV
        state["ev"] += 1
        xh = roph[i]
        xl = ropl[i]
        xf = evf[i]
        xc = evc[i]
        sch.op(ACT, lambda e: e.activation(out=xh[0:nparts, :], in_=src_ps, func=AF.Copy),
               reads=[src_buf], writes=[B("roph", i)])
        sch.op(DVE, lambda e: e.tensor_tensor(out=xl[0:nparts, :], in0=src_ps, in1=xh[0:nparts, :], op=ALU.subtract),
               reads=[src_buf, B("roph", i)], writes=[B("ropl", i)])
        sch.op(DVE, lambda e: e.tensor_tensor(out=xc[0:nparts, :], in0=src_ps, in1=P["tabC"][0:nparts, :], op=ALU.mult),
               reads=[src_buf, B("tabC")], writes=[B("evc", i)])

        def stage2():
            bk = next_bank()
            sch.op(PE, lambda e: e.matmul(ps[0:nparts, bk, :], lhsT=cst["perm_mb" if perm is cst["perm_m"] else "perm_db"][0:nparts, 0:nparts],
                                          rhs=xh[0:nparts, :], start=True, stop=False),
                   reads=[B("roph", i)], writes=[PS[bk]], sig=False)
            sch.op(PE, lambda e: e.matmul(ps[0:nparts, bk, :], lhsT=cst["perm_mb" if perm is cst["perm_m"] else "perm_db"][0:nparts, 0:nparts],
                                          rhs=xl[0:nparts, :], start=False, stop=True),
                   reads=[B("ropl", i)], writes=[PS[bk]], sig=True)
            sch.op(DVE, lambda e: e.tensor_tensor(out=xf[0:nparts, 0:TT], in0=ps[0:nparts, bk, :], in1=P["tabS"][0:nparts, :], op=ALU.mult),
                   reads=[PS[bk], B("tabS")], writes=[B("evf", i)])
            sch.op(DVE, lambda e: e.tensor_tensor(out=out_bf, in0=xc[0:nparts, :], in1=xf[0:nparts, 0:TT], op=ALU.add),
                   reads=[B("evf", i), B("evc", i)], writes=[out_buf])
            after()
        return stage2

    def next_evb():
        i = state["evb"] % NEB
        state["evb"] += 1
        return i

    def load_x(layer, t, from_xs=False):
        src_ = xs if (layer > 0 or from_xs) else x_in
        sch.dma(SP, P["xt"][:], src_[:, :, t * TT:(t + 1) * TT].rearrange("c p t -> p c t"),
                reads=[B("xs", t)], writes=[B("xt")])

    def evac_to(dst_ap, dst_buf, nparts=128, shared=False):
        def f(oc, pap, pbuf):
            i = next_evb()
            sch.op(ACT, lambda e: e.activation(out=evb[i][0:nparts, :], in_=pap, func=AF.Copy),
                   reads=[pbuf], writes=[B("evb", i)])
            if shared:
                storeP(lambda s_: dst_ap(oc, s_), evb[i][0:nparts, :], reads=[B("evb", i)], writes=[dst_buf(oc)])
            else:
                sch.dma(POOL, dst_ap(oc), evb[i][0:nparts, :], reads=[B("evb", i)], writes=[dst_buf(oc)])
            return None
        return f

    def v_proj(name, n_cb, KC, lhs_fn, lhs_bufs, dst_fn, dst_buf_fn):
        for cb in range(n_cb):
            banks = [next_bank() for _ in range(4)]
            k0 = 0
            while k0 < KC:
                kn = min(4, KC - k0)
                wv, wb = load_stage(name, cb, k0, kn, width=512)
                for sbk in range(4):
                    for kk in range(kn):
                        kc = k0 + kk
                        last = kc == KC - 1
                        sch.op(PE, lambda e, kk=kk, kc=kc, sbk=sbk, last=last: e.matmul(
                            ps[:, banks[sbk], :], lhsT=lhs_fn(kc, sbk), rhs=wv[:, kk, :], start=(kc == 0), stop=last),
                            reads=[wb] + lhs_bufs, writes=[PS[banks[sbk]]], sig=(kk == kn - 1))
                k0 += kn
            for sbk in range(4):
                i = next_evb()
                bk = banks[sbk]
                sch.op(ACT, lambda e, i=i, bk=bk: e.activation(out=evb[i][:], in_=ps[:, bk, :], func=AF.Copy),
                       reads=[PS[bk]], writes=[B("evb", i)])
                storeP(lambda s_, cb=cb, sbk=sbk: dst_fn(cb, sbk, s_), evb[i][:].rearrange("p (h j) -> p h j", j=128),
                       reads=[B("evb", i)], writes=[dst_buf_fn(cb, sbk)])

    def mla_phase_a(layer):
        j = layer // 2
        xt, small, smallb = P["xt"], P["small"], P["smallb"]
        for t in range(NT):
            tsl = slice(t * TT, (t + 1) * TT)
            load_x(layer, t)
            rmsnorm(lambda c: xt[:, c, :], [B("xt")], DC, D, lambda c: s_gattn[:, layer, c:c + 1],
                    lambda c: ht[:, c, :], [B("ht")])
            rope_tables(t, "m")

            def ev_small(oc, pap, pbuf):
                sch.op(ACT, lambda e: e.activation(out=small[:, oc, :], in_=pap, func=AF.Copy),
                       reads=[pbuf], writes=[B("small")])
            proj_fm(f"L{layer}_wqa", 4, 16, lambda kc: ht[:, kc, :], [B("ht")], ev_small)
            rmsnorm(lambda c: small[:, c, :], [B("small")], 4, 512, lambda c: s_gqn[:, j, c:c + 1],
                    lambda c: smallb[:, c, :], [B("smallb")])
            proj_fm(f"L{layer}_wqbn", 16, 4, lambda kc: smallb[:, kc, :], [B("smallb")],
                    evac_to(lambda oc: sc_qn[oc, :, tsl], lambda oc: B("sc_qn", oc, t)))

            def ev_qpe(oc, pap, pbuf):
                i = next_evb()

                def after():
                    for hh_ in range(2):
                        sch.dma(POOL, sc_qp[2 * oc + hh_, :, tsl], evb[i][64 * hh_:64 * hh_ + 64, :],
                                reads=[B("evb", i)], writes=[B("sc_qp", 2 * oc + hh_, t)])
                return rope_apply(pap, pbuf, 128, cst["perm_m"], evb[i][:], B("evb", i), after)
            proj_fm(f"L{layer}_wqbp", 8, 4, lambda kc: smallb[:, kc, :], [B("smallb")], ev_qpe)
            proj_fm(f"L{layer}_wkva", 4, 16, lambda kc: ht[:, kc, :], [B("ht")], ev_small)

            def ev_kpe(oc, pap, pbuf):
                i = next_evb()

                def after():
                    storeP(lambda s_: sh_kp[s_, :, tsl], evb[i][0:64, :],
                           reads=[B("evb", i)], writes=[B("sh_kp", t)])
                return rope_apply(pap, pbuf, 64, cst["perm_m"], evb[i][0:64, :], B("evb", i), after)
            proj_fm(f"L{layer}_wkpe", 1, 16, lambda kc: ht[:, kc, :], [B("ht")], ev_kpe, width=64)
            rmsnorm(lambda c: small[:, c, :], [B("small")], 4, 512, lambda c: s_gkvn[:, j, c:c + 1],
                    lambda c: smallb[:, c, :], [B("smallb")])
            proj_fm(f"L{layer}_wkbk", 16, 4, lambda kc: smallb[:, kc, :], [B("smallb")],
                    evac_to(lambda oc, s_: sh_kn[s_, oc, :, tsl], lambda oc: B("sh_kn", oc, t), shared=True))
            v_proj(f"L{layer}_wkbv", 4, 4, lambda kc, sbk: smallb[:, kc, sbk * 128:(sbk + 1) * 128], [B("smallb")],
                   lambda cb, sbk, s_: sh_v[s_ * 16 + 4 * cb:s_ * 16 + 4 * cb + 4, :, t * 4 + sbk, :].rearrange("h p j -> p h j"),
                   lambda cb, sbk: B("sh_v", t, cb, sbk))

    def mla_phase_b(layer, flag_k):
        scale = (128 + 64) ** -0.5
        poll_flag(flag_k)
        akp = P["akp"]
        sch.op(POOL, lambda e: e.memset(akp[64:128, :], 0.0), writes=[B("akpz")])
        for i_ in range(2):
            sch.op(POOL, lambda e, i_=i_: e.memset(P[f"aqp{i_}"][64:128, :], 0.0), writes=[B("aqpz", i_)])
        sch.dma(SP, akp[0:64, 0:T], sh_kp[0], writes=[B("akp")])
        loadS(akp[0:64, T:TS], lambda s_: sh_kp[s_],
              reads=[B("sh_kp", t) for t in range(NT)], writes=[B("akp")], add=True)
        its = []
        for h in range(16):
            for qi in range(NT):
                nkb = HB + 4 * qi + 4
                for kb in range(nkb):
                    its.append((h, qi, kb, nkb))

        def load_head(h):
            bi = h % 2
            akn, akv = P[f"akn{bi}"], P[f"akv{bi}"]
            sch.dma(SP, akn[:, 0:T], sh_kn[0, h], writes=[B("akn", bi)])
            loadS(akn[:, T:TS], lambda s_: sh_kn[s_, h],
                  reads=[B("sh_kn", h, t) for t in range(NT)], writes=[B("akn", bi)], add=True)
            sch.dma(SP, akv[:, 0:HB, :], sh_v[h], writes=[B("akv", bi)])
            loadS(akv[:, HB:NKB, :], lambda s_: sh_v[s_ * 16 + h],
                  reads=[B("sh_v", t, h // 4, x_) for t in range(NT) for x_ in range(4)],
                  writes=[B("akv", bi)], add=True)

        def load_q(h, qi):
            qb = (h * NT + qi) % 2
            qsl = slice(qi * TT, (qi + 1) * TT)
            sch.dma(SP, P[f"aqn{qb}"][:], sc_qn[h, :, qsl], reads=[B("sc_qn", h, qi)], writes=[B("aqn", qb)])
            sch.dma(SP, P[f"aqp{qb}"][0:64, :], sc_qp[h, :, qsl], reads=[B("sc_qp", h, qi)], writes=[B("aqp", qb)])

        def s1(it):
            h, qi, kb, nkb = it
            bi = h % 2
            qb = (h * NT + qi) % 2
            akn, akv = P[f"akn{bi}"], P[f"akv{bi}"]
            aqn, aqp = P[f"aqn{qb}"], P[f"aqp{qb}"]
            if h == 0 and qi == 0 and kb == 0:
                load_head(0)
                load_q(0, 0)
            if kb == 0:
                nq = h * NT + qi + 1
                if nq < 16 * NT:
                    load_q(nq // NT, nq % NT)
            if qi == 0 and kb == 3 and h + 1 < 16:
                load_head(h + 1)
            bk = next_bank(0, 4)
            ksl = slice(kb * 128, (kb + 1) * 128)
            pace(0.9)
            sch.op(PE, lambda e: e.matmul(ps[:, bk, :], lhsT=akn[:, ksl], rhs=aqn[:], start=True, stop=False),
                   reads=[B("akn", bi), B("aqn", qb)], writes=[PS[bk]], sig=False)
            sch.op(PE, lambda e: e.matmul(ps[:, bk, :], lhsT=akp[:, ksl], rhs=aqp[:], start=False, stop=True),
                   reads=[B("akp"), B("akpz"), B("aqp", qb), B("aqpz", qb)], writes=[PS[bk]], sig=True)
            pi = state["ev"] % NPT
            state["ev"] += 1
            if kb < HB:
                sch.op(ACT, lambda e: e.activation(out=apt[pi][:], in_=ps[:, bk, :], func=AF.Exp,
                                                   bias=s_hb[:, 0:1], scale=scale),
                       reads=[PS[bk]], writes=[B("apt", pi)])
            else:
                sch.op(ACT, lambda e: e.activation(out=apt[pi][:], in_=ps[:, bk, :], func=AF.Exp, scale=scale),
                       reads=[PS[bk]], writes=[B("apt", pi)])
            a = kb - HB - 4 * qi
            if a >= 0:
                sch.op(DVE, lambda e: e.tensor_tensor(out=apt[pi][:], in0=apt[pi][:], in1=cst["mask_m"][:, a, :],
                                                      op=ALU.mult),
                       reads=[B("apt", pi)], writes=[B("apt", pi)])
            return pi

        def s2(it, pi):
            h, qi, kb, nkb = it
            bi = h % 2
            akv = P[f"akv{bi}"]
            par = (h * NT + qi) % 2
            ob, db = 4 + par, 6 + par
            last = kb == nkb - 1
            sch.op(PE, lambda e: e.matmul(ps[:, ob, :], lhsT=akv[:, kb, :], rhs=apt[pi][:], start=(kb == 0), stop=last),
                   reads=[B("akv", bi), B("apt", pi)], writes=[PS[ob]], sig=False)
            sch.op(PE, lambda e: e.matmul(ps[:, db, :], lhsT=cst["ones_b"][:], rhs=apt[pi][:], start=(kb == 0), stop=last),
                   reads=[B("apt", pi)], writes=[PS[db]], sig=True)
            if last:
                qsl = slice(qi * TT, (qi + 1) * TT)
                sch.op(DVE, lambda e: e.reciprocal(out=arec[:], in_=ps[:, db, :]), reads=[PS[db]], writes=[B("arec")])
                i = next_evb()
                sch.op(DVE, lambda e: e.tensor_tensor(out=evb[i][:], in0=ps[:, ob, :], in1=arec[:], op=ALU.mult),
                       reads=[PS[ob], B("arec")], writes=[B("evb", i)])
                sch.dma(POOL, sc_o[h, :, qsl], evb[i][:], reads=[B("evb", i)], writes=[B("sc_o", h, qi)])

        LA = 2
        pis = {}
        n = len(its)
        for i in range(min(LA, n)):
            pis[i] = s1(its[i])
        for i in range(n):
            if i + LA < n:
                pis[i + LA] = s1(its[i + LA])
            s2(its[i], pis.pop(i))

    def phase_c1(layer, wo_name):
        ot = P["ot"]
        src_ = xs if layer > 0 else x_in

        def loads(t):
            tsl_ = slice(t * TT, (t + 1) * TT)
            sch.dma(SP, ot[:], sc_o[:, :, tsl_].rearrange("c p t -> p c t"),
                    reads=[B("sc_o", h, t) for h in range(16)], writes=[B("ot")])
            sch.dma(SP, P[f"xt{t % 2}"][:], src_[:, :, tsl_].rearrange("c p t -> p c t"),
                    reads=[B("xs", t)], writes=[B("xt", t % 2)])

        loads(0)
        for t in range(NT):
            tsl = slice(t * TT, (t + 1) * TT)
            xt = P[f"xt{t % 2}"]
            xb = B("xt", t % 2)

            def ev_res(oc, pap, pbuf, xt=xt, xb=xb):
                sch.op(DVE, lambda e: e.tensor_tensor(out=xt[:, oc, :], in0=pap, in1=xt[:, oc, :], op=ALU.add),
                       reads=[pbuf, xb], writes=[xb])
            proj_fm(wo_name, 16, 16, lambda kc: ot[:, kc, :], [B("ot")], ev_res)
            if t + 1 < NT:
                loads(t + 1)
            rmsnorm(lambda c: xt[:, c, :], [xb], DC, D, lambda c: s_gffn[:, layer, c:c + 1],
                    lambda c: ht[:, c, :], [B("ht")])
            sch.dma(POOL, xs[:, :, tsl].rearrange("c p t -> p c t"), xt[:], reads=[xb], writes=[B("xs", t)])
            sch.dma(POOL, sc_h2[:, :, tsl].rearrange("c p t -> p c t"), ht[:], reads=[B("ht")], writes=[B("sc_h2", t)])
            if t == NT - 1:
                storeP(lambda s_: sh_h2[s_].rearrange("c p t -> p c t"), ht[:, :, TT - 2:TT],
                       reads=[B("ht")], writes=[B("sh_h2")])

    def phase_c2(layer, last_layer, flag_k):
        xt, gt = P["xt"], P["gt"]
        name = f"L{layer}_wup"
        poll_flag(flag_k)
        sch.dma(SP, hh[:], sh_h2[0].rearrange("c p t -> p c t"), writes=[B("hh")])
        sch.op(DVE, lambda e: e.tensor_scalar(out=hh2[:], in0=hh[:], scalar1=s_hb[:, 1:2], scalar2=None, op0=ALU.mult),
               reads=[B("hh")], writes=[B("hh2")])
        for t in range(NT):
            tsl = slice(t * TT, (t + 1) * TT)
            load_x(layer, t, from_xs=True)
            sch.dma(SP, ht[:], sc_h2[:, :, tsl].rearrange("c p t -> p c t"), reads=[B("sc_h2", t)], writes=[B("ht")])

            def conv_chunk(oc, pap, pbuf, dst, dst_buf):
                i = state["ev"] % NEV
                state["ev"] += 1
                u = evf[i]
                ub = B("evf", i)
                sch.op(POOL, lambda e: e.tensor_copy(out=u[:, 0:2], in_=uhalo[:, oc, :]),
                       reads=[B("uhalo", oc)], writes=[ub])
                sch.op(ACT, lambda e: e.activation(out=u[:, 2:TT + 2], in_=pap, func=AF.Copy),
                       reads=[pbuf], writes=[ub])
                sch.op(POOL, lambda e: e.tensor_copy(out=uhalo[:, oc, :], in_=u[:, TT:TT + 2]),
                       reads=[ub], writes=[B("uhalo", oc)])
                sch.op(ACT, lambda e: e.activation(out=dst, in_=u[:, 2:TT + 2], func=AF.Identity,
                                                   bias=s_cb[:, layer, oc:oc + 1], scale=s_cw[:, layer, 2, oc:oc + 1]),
                       reads=[ub], writes=[dst_buf])
                sch.op(DVE, lambda e: e.scalar_tensor_tensor(out=dst, in0=u[:, 1:TT + 1], scalar=s_cw[:, layer, 1, oc:oc + 1],
                                                             in1=dst, op0=ALU.mult, op1=ALU.add),
                       reads=[ub, dst_buf], writes=[dst_buf])
                sch.op(DVE, lambda e: e.scalar_tensor_tensor(out=dst, in0=u[:, 0:TT], scalar=s_cw[:, layer, 0, oc:oc + 1],
                                                             in1=dst, op0=ALU.mult, op1=ALU.add),
                       reads=[ub, dst_buf], writes=[dst_buf])

            def gate_finish(jc, res):
                ci = jc % 2
                cg = cgate[ci]
                conv_chunk(jc, ps[:, res["g"], :], PS[res["g"]], cg[:], B("cgate", ci))
                sch.op(ACT, lambda e: e.activation(out=cg[:], in_=cg[:], func=AF.Silu),
                       reads=[B("cgate", ci)], writes=[B("cgate", ci)])
                i = state["ev"] % NEV
                vb = evc[i]
                conv_chunk(FC + jc, ps[:, res["v"], :], PS[res["v"]], vb[:], B("evc", i))
                sch.op(DVE, lambda e: e.tensor_tensor(out=gt[:, jc, :], in0=cg[:], in1=vb[:], op=ALU.mult),
                       reads=[B("cgate", ci), B("evc", i)], writes=[B("gt")])

            pend = None
            for jc in range(FC):
                res = {}
                for which, oc in (("g", jc), ("v", FC + jc)):
                    bk = next_bank()
                    wv, wb = load_stage(name, oc, 0, 16)
                    if t == 0:
                        hbk = next_bank()
                        for kc in range(16):
                            sch.op(PE, lambda e, kc=kc: e.matmul(ps[:, hbk, 0:2], lhsT=wv[:, kc, :], rhs=hh2[:, kc, :],
                                                                 start=(kc == 0), stop=(kc == 15)),
                                   reads=[wb, B("hh2")], writes=[PS[hbk]], sig=(kc == 15))
                        sch.op(ACT, lambda e, oc=oc: e.activation(out=uhalo[:, oc, :], in_=ps[:, hbk, 0:2], func=AF.Copy),
                               reads=[PS[hbk]], writes=[B("uhalo", oc)])
                    for kc in range(16):
                        sch.op(PE, lambda e, kc=kc: e.matmul(ps[:, bk, :], lhsT=wv[:, kc, :], rhs=ht[:, kc, :],
                                                             start=(kc == 0), stop=(kc == 15)),
                               reads=[wb, B("ht")], writes=[PS[bk]], sig=(kc == 15))
                    res[which] = bk
                if pend is not None:
                    gate_finish(*pend)
                pend = (jc, res)
            gate_finish(*pend)

            def ev_res2(oc, pap, pbuf):
                sch.op(DVE, lambda e: e.tensor_tensor(out=xt[:, oc, :], in0=pap, in1=xt[:, oc, :], op=ALU.add),
                       reads=[pbuf, B("xt")], writes=[B("xt")])
            proj_fm(f"L{layer}_wdn", 16, FC, lambda kc: gt[:, kc, :], [B("gt")], ev_res2)
            if last_layer and final_norm:
                bk = next_bank()
                for c in range(DC):
                    q = sqb[c % 2]
                    sch.op(ACT, lambda e, c=c, q=q: e.activation(out=q[:], in_=xt[:, c, :], func=AF.Square),
                           reads=[B("xt")], writes=[B("sqb", c % 2)])
                    sch.op(PE, lambda e, c=c, q=q: e.matmul(ps[:, bk, :], lhsT=cst["ones_f"][:], rhs=q[:],
                                                            start=(c == 0), stop=(c == DC - 1)),
                           reads=[B("sqb", c % 2)], writes=[PS[bk]], sig=True)
                sch.op(ACT, lambda e: e.activation(out=rsb[:], in_=ps[:, bk, :], func=AF.Sqrt, bias=eps_t[:], scale=1.0 / D),
                       reads=[PS[bk]], writes=[B("rsb")])
                sch.op(DVE, lambda e: e.reciprocal(out=rsb[:], in_=rsb[:]), reads=[B("rsb")], writes=[B("rsb")])
                for c in range(DC):
                    sch.op(DVE, lambda e, c=c: e.scalar_tensor_tensor(
                        out=xt[:, c, :], in0=xt[:, c, :], scalar=s_gfin[:, c:c + 1], in1=rsb[:],
                        op0=ALU.mult, op1=ALU.mult), reads=[B("xt"), B("rsb")], writes=[B("xt")])
            if last_layer:
                sch.dma(SP, y_out[:, :, tsl].rearrange("c p t -> p c t"), xt[:], reads=[B("xt")], writes=[B("y", t)])
            else:
                sch.dma(SP, xs[:, :, tsl].rearrange("c p t -> p c t"), xt[:], reads=[B("xt")], writes=[B("xs", t)])

    DIL = (1, 4, 16)

    def dil_phase_a(layer):
        xt = P["xt"]
        for t in range(NT):
            tsl = slice(t * TT, (t + 1) * TT)
            load_x(layer, t)
            rmsnorm(lambda c: xt[:, c, :], [B("xt")], DC, D, lambda c: s_gattn[:, layer, c:c + 1],
                    lambda c: ht[:, c, :], [B("ht")])
            rope_tables(t, "d")

            def ev_qk(oc, pap, pbuf):
                g, rem = divmod(oc, 32)
                qk, h = divmod(rem, 16)
                i = next_evb()

                def after():
                    if qk == 0:
                        sch.dma(POOL, sc_dq[g, h, :, tsl], evb[i][:], reads=[B("evb", i)], writes=[B("sc_dq", g, h, t)])
                    else:
                        storeP(lambda s_: sh_dk[s_, g * 16 + h, :, tsl], evb[i][:],
                               reads=[B("evb", i)], writes=[B("sh_dk", g, h, t)])
                return rope_apply(pap, pbuf, 128, cst["perm_d"], evb[i][:], B("evb", i), after)
            proj_fm(f"L{layer}_wqk", 96, 16, lambda kc: ht[:, kc, :], [B("ht")], ev_qk)
            v_proj(f"L{layer}_wv", 12, 16, lambda kc, sbk: ht[:, kc, sbk * 128:(sbk + 1) * 128], [B("ht")],
                   lambda cb, sbk, s_: sh_dv[s_ * 48 + (cb // 4) * 16 + 4 * (cb % 4):s_ * 48 + (cb // 4) * 16 + 4 * (cb % 4) + 4,
                                             t * TT + sbk * 128:t * TT + sbk * 128 + 128, :].rearrange("h p j -> p h j"),
                   lambda cb, sbk: B("sh_dv", cb // 4, t, cb % 4, sbk))

    def dil_phase_b(layer, flag_k):
        scale = 128 ** -0.5
        poll_flag(flag_k)
        oacc, dacc = P["oacc"], P["dacc"]
        its = []
        for h in range(16):
            for g, d in enumerate(DIL):
                nbt = TS // (128 * d)
                for r in range(d):
                    for b in range(nbt // 2, nbt):
                        its.append((h, g, d, r, b, nbt))
        last_of_head = {}
        for idx, it in enumerate(its):
            last_of_head[it[0]] = idx

        def load_group(h, g):
            d = DIL[g]
            nbt = TS // (128 * d)
            bi = (h * 3 + g) % 3
            akn, akv, aqf = P[f"akn{bi}"], P[f"akv{bi}"], P[f"aqf{bi}"]
            gh = g * 16 + h
            sch.dma(SP, akn[:, 0:T], sh_dk[0, gh], writes=[B("akn", bi)])
            loadS(akn[:, T:TS], lambda s_: sh_dk[s_, gh],
                  reads=[B("sh_dk", g, h, t) for t in range(NT)], writes=[B("akn", bi)], add=True)
            sch.dma(SP, aqf[:], sc_dq[g, h], reads=[B("sc_dq", g, h, t) for t in range(NT)], writes=[B("aqf", bi)])
            hb_n = nbt // 2
            vreads = [B("sh_dv", g, t, h // 4, s_) for t in range(NT) for s_ in range(4)]
            for rr in range(d):
                srch = sh_dv[gh].rearrange("(b k dd) j -> dd k b j", k=128, dd=d)[rr]
                sch.dma(SP, akv[:, rr * nbt + hb_n - 1:rr * nbt + hb_n, :], srch[:, hb_n - 1:hb_n, :],
                        writes=[B("akv", bi)], add=not (rr == 0))
            for rr in range(d):
                step = min(hb_n, 8)
                for b0 in range(0, hb_n, step):
                    loadS(akv[:, rr * nbt + hb_n + b0:rr * nbt + hb_n + b0 + step, :],
                          lambda s_, rr=rr, b0=b0: sh_dv[s_ * 48 + gh].rearrange(
                              "(b k dd) j -> dd k b j", k=128, dd=d)[rr][:, b0:b0 + step, :],
                          reads=vreads if (rr == 0 and b0 == 0) else [], writes=[B("akv", bi)], add=True)

        it_local = {}
        _cnt = {}
        for it_ in its:
            k_ = (it_[0], it_[1])
            it_local[id(it_)] = _cnt.get(k_, 0)
            _cnt[k_] = _cnt.get(k_, 0) + 1

        def s1(it):
            h, g, d, r, b, nbt = it
            bi = (h * 3 + g) % 3
            akn, akv, aqf = P[f"akn{bi}"], P[f"akv{bi}"], P[f"aqf{bi}"]
            gi_ = h * 3 + g
            if gi_ == 0 and r == 0 and b == nbt // 2:
                load_group(0, 0)
                load_group(0, 1)
            if it_local[id(it)] == 4 and gi_ + 2 < 48:
                load_group((gi_ + 2) // 3, (gi_ + 2) % 3)
            c0 = 128 * b * d + r
            kcols = slice(c0, c0 + 127 * d + 1, d)
            pc0 = c0 - 128 * d
            pcols = slice(pc0, pc0 + 127 * d + 1, d)
            qcols = slice(c0 - T, c0 - T + 127 * d + 1, d)
            bk = next_bank(0, 4)
            pace(1.4)
            sch.op(PE, lambda e: e.matmul(ps[:, bk, 0:128], lhsT=akn[:, pcols], rhs=aqf[:, qcols], start=True, stop=True),
                   reads=[B("akn", bi), B("aqf", bi)], writes=[PS[bk]], sig=False)
            sch.op(PE, lambda e: e.matmul(ps[:, bk, 128:256], lhsT=akn[:, kcols], rhs=aqf[:, qcols], start=True, stop=True),
                   reads=[B("akn", bi), B("aqf", bi)], writes=[PS[bk]], sig=True)
            pi = state["ev"] % NPT
            state["ev"] += 1
            if b == nbt // 2:
                sch.op(ACT, lambda e: e.activation(out=apt[pi][:, 0:128], in_=ps[:, bk, 0:128], func=AF.Exp,
                                                   bias=s_hb[:, 0:1], scale=scale),
                       reads=[PS[bk]], writes=[B("apt", pi)])
                sch.op(ACT, lambda e: e.activation(out=apt[pi][:, 128:256], in_=ps[:, bk, 128:256], func=AF.Exp, scale=scale),
                       reads=[PS[bk]], writes=[B("apt", pi)], add=True)
            else:
                sch.op(ACT, lambda e: e.activation(out=apt[pi][:, 0:256], in_=ps[:, bk, 0:256], func=AF.Exp, scale=scale),
                       reads=[PS[bk]], writes=[B("apt", pi)])
            sch.op(DVE, lambda e: e.tensor_tensor(out=apt[pi][:, 0:256], in0=apt[pi][:, 0:256],
                                                  in1=cst["mask_d"][:].rearrange("p a q -> p (a q)"), op=ALU.mult),
                   reads=[B("apt", pi)], writes=[B("apt", pi)])
            return pi

        cnt = {"i": 0}

        def s2(it, pi, idx):
            h, g, d, r, b, nbt = it
            bi = (h * 3 + g) % 3
            akv = P[f"akv{bi}"]
            par = cnt["i"] % 2
            cnt["i"] += 1
            ob, db = 4 + par, 6 + par
            n = r * nbt + b
            c0 = 128 * b * d + r
            qcols = slice(c0 - T, c0 - T + 127 * d + 1, d)
            sch.op(PE, lambda e: e.matmul(ps[:, ob, 0:128], lhsT=akv[:, n - 1, :], rhs=apt[pi][:, 0:128], start=True, stop=False),
                   reads=[B("akv", bi), B("apt", pi)], writes=[PS[ob]], sig=False)
            sch.op(PE, lambda e: e.matmul(ps[:, ob, 0:128], lhsT=akv[:, n, :], rhs=apt[pi][:, 128:256], start=False, stop=True),
                   reads=[B("akv", bi), B("apt", pi)], writes=[PS[ob]], sig=False)
            sch.op(PE, lambda e: e.matmul(ps[:, db, 0:128], lhsT=cst["ones_b"][:], rhs=apt[pi][:, 0:128], start=True, stop=False),
                   reads=[B("apt", pi)], writes=[PS[db]], sig=False)
            sch.op(PE, lambda e: e.matmul(ps[:, db, 0:128], lhsT=cst["ones_b"][:], rhs=apt[pi][:, 128:256], start=False, stop=True),
                   reads=[B("apt", pi)], writes=[PS[db]], sig=True)
            if g == 0:
                regs = [(b - nbt // 2) // 4]
            elif g == 1:
                regs = [b - nbt // 2]
            else:
                regs = list(range(NT))
            ob_ = [B("oacc", q_) for q_ in regs]
            db_ = [B("dacc", q_) for q_ in regs]
            if g == 0:
                sch.op(ACT, lambda e: e.activation(out=oacc[:, qcols], in_=ps[:, ob, 0:128], func=AF.Copy),
                       reads=[PS[ob]], writes=ob_)
                sch.op(ACT, lambda e: e.activation(out=dacc[:, qcols], in_=ps[:, db, 0:128], func=AF.Copy),
                       reads=[PS[db]], writes=db_)
            else:
                sch.op(DVE, lambda e: e.tensor_tensor(out=oacc[:, qcols], in0=ps[:, ob, 0:128], in1=oacc[:, qcols], op=ALU.add),
                       reads=[PS[ob]] + ob_, writes=ob_)
                sch.op(DVE, lambda e: e.tensor_tensor(out=dacc[:, qcols], in0=ps[:, db, 0:128], in1=dacc[:, qcols], op=ALU.add),
                       reads=[PS[db]] + db_, writes=db_)
            if last_of_head[h] == idx:
                for qi in range(NT):
                    qsl = slice(qi * TT, (qi + 1) * TT)
                    sch.op(DVE, lambda e: e.reciprocal(out=dacc[:, qsl], in_=dacc[:, qsl]),
                           reads=[B("dacc", qi)], writes=[B("dacc", qi)])
                    i = next_evb()
                    sch.op(DVE, lambda e: e.tensor_tensor(out=evb[i][:], in0=oacc[:, qsl], in1=dacc[:, qsl], op=ALU.mult),
                           reads=[B("oacc", qi), B("dacc", qi)], writes=[B("evb", i)])
                    sch.dma(POOL, sc_o[h, :, qsl], evb[i][:], reads=[B("evb", i)], writes=[B("sc_o", h, qi)])

        LA = 3
        pis = {}
        n = len(its)
        for i in range(min(LA, n)):
            pis[i] = s1(its[i])
        for i in range(n):
            if i + LA < n:
                pis[i + LA] = s1(its[i + LA])
            s2(its[i], pis.pop(i), i)

    def run_phase(kind, tag, fn, *args):
        with ExitStack() as pes:
            phase_alloc(pes, kind)
            with nc.named_scope(tag):
                fn(*args)
                barrier()

    pump_casts(24)
    for layer in range(depth):
        last = layer == depth - 1
        cast_pos["layer"] = layer
        if layer % 2 == 0:
            run_phase("A", f"L{layer}_A", mla_phase_a, layer)
            k = post_flag()
            run_phase("BM", f"L{layer}_B", mla_phase_b, layer, k)
        else:
            run_phase("A", f"L{layer}_A", dil_phase_a, layer)
            k = post_flag()
            run_phase("BD", f"L{layer}_B", dil_phase_b, layer, k)
        if layer > 0:
            poll_done(layer - 1)
        run_phase("C1", f"L{layer}_C1", phase_c1, layer, f"L{layer}_wo")
        k = post_flag()
        run_phase("C2", f"L{layer}_C2", phase_c2, layer, last, k)
        if not last:
            post_done(layer)

    for t in range(NT):
        b = sch.bufs.get(("y", t))
        if b is not None:
            for tk in b.w:
                sch._wait(SP, tk)
    for E in (PE, ACT, DVE, POOL):
        if E.count > 0:
            sch._wait(SP, (E.sem, E.count, E.name))
    for Q in (SP, POOL, ACT):
        for i, sem in enumerate(Q.dma_sems):
            if Q.dma_cnt[i] > 0:
                sch._wait(SP, (sem, Q.dma_cnt[i], f"{Q.name}.d{i}"))
    es.close()
    return nc


_PROGRAM_CACHE = {}
_LAST = {}


def kernel(x, positions, attn_norm, ffn_norm, final_norm,
           mla_wq_a, mla_q_norm, mla_wq_b, mla_wkv_a, mla_kv_norm, mla_wkv_b, mla_wo,
           dil_w_in, dil_wo, ffn_w_up, ffn_conv_w, ffn_conv_b, ffn_w_down):
    inp = dict(x=x, positions=positions, attn_norm=attn_norm, ffn_norm=ffn_norm, final_norm=final_norm,
               mla_wq_a=mla_wq_a, mla_q_norm=mla_q_norm, mla_wq_b=mla_wq_b, mla_wkv_a=mla_wkv_a,
               mla_kv_norm=mla_kv_norm, mla_wkv_b=mla_wkv_b, mla_wo=mla_wo, dil_w_in=dil_w_in, dil_wo=dil_wo,
               ffn_w_up=ffn_w_up, ffn_conv_w=ffn_conv_w, ffn_conv_b=ffn_conv_b, ffn_w_down=ffn_w_down)
    inp = {k: np.asarray(v) for k, v in inp.items()}
    return run_model(inp, DEPTH, N_CORES)


def common_inputs(inp, depth):
    shared = {}
    shared["g_attn"] = np.ascontiguousarray(inp["attn_norm"].reshape(DEPTH, DC, 128).transpose(2, 0, 1))
    shared["g_ffn"] = np.ascontiguousarray(inp["ffn_norm"].reshape(DEPTH, DC, 128).transpose(2, 0, 1))
    shared["g_fin"] = lay_vec(inp["final_norm"])
    shared["g_qn"] = np.ascontiguousarray(inp["mla_q_norm"].reshape(2, 4, 128).transpose(2, 0, 1))
    shared["g_kvn"] = np.ascontiguousarray(inp["mla_kv_norm"].reshape(2, 4, 128).transpose(2, 0, 1))
    shared["conv_w"] = np.ascontiguousarray(inp["ffn_conv_w"].reshape(DEPTH, 3, 88, 128).transpose(3, 0, 1, 2))
    shared["conv_b"] = np.ascontiguousarray(inp["ffn_conv_b"].reshape(DEPTH, 88, 128).transpose(2, 0, 1))
    shared.update(const_inputs())
    shared.update(prep_weights(inp, depth))
    return shared


def run_model(inp, depth, n_cores, trace=False, **bkw):
    key = (depth, tuple(sorted(bkw.items())))
    if key not in _PROGRAM_CACHE:
        _PROGRAM_CACHE[key] = build_program(depth=depth, **bkw)
    nc = _PROGRAM_CACHE[key]
    shared = common_inputs(inp, depth)
    nonce = int(np.random.default_rng().integers(1, 2 ** 30))
    tok = (nonce + np.arange(NFLAG)).astype(np.int32)[None, :]
    tok2 = (nonce + 1000 + np.arange(NFLAG)).astype(np.int32)[None, :]
    in_maps = []
    for c in range(n_cores):
        b, role = divmod(c, 2)
        sl = slice(role * TC, (role + 1) * TC)
        m = dict(shared)
        m["xT"] = np.ascontiguousarray(inp["x"][b, sl].T.reshape(DC, 128, TC))
        m["pos"] = np.ascontiguousarray(np.broadcast_to(inp["positions"][b, sl].astype(np.int32)[None, :], (128, TC)))
        m["role"] = np.array([[role]], np.int32)
        m["tok"] = tok
        m["tok2"] = tok2
        hb = np.zeros((128, 2), np.float32)
        hb[:, 0] = -30000.0 if role == 0 else 0.0
        hb[:, 1] = 0.0 if role == 0 else 1.0
        m["hbias"] = hb
        in_maps.append(m)
    if trace:
        res = run_bass_kernel_spmd(nc, in_maps, core_ids=list(range(n_cores)), trace=True)
    else:
        res = run_bass_kernel_spmd(nc, in_maps, core_ids=list(range(n_cores)))
    _LAST["res"] = res
    outs = []
    for b in range(n_cores // 2):
        halves = [np.ascontiguousarray(res.results[2 * b + r]["yT"].reshape(D, TC).T) for r in range(2)]
        outs.append(np.concatenate(halves, axis=0))
    return np.stack(outs, axis=0).astype(np.float32)
```
